# Optimizing a Trainium2 kernel written in Bass

```python
import math
import jax, jax.numpy as jnp
from jax import lax
import numpy as np

D_MODEL = 2048
BATCH = 4
SEQ = 4096
DEPTH = 4

N_META = 16
BLOCK = 128
WINDOW = 128
PAD_FRONT = BLOCK - N_META
MIX_WIDTH = D_MODEL
ATT_HEAD_DIM = 128
ATT_WIDTH = MIX_WIDTH // 2
ATT_HEADS = ATT_WIDTH // ATT_HEAD_DIM
ATT_KV_HEADS = 2
KV_WIDTH = ATT_KV_HEADS * ATT_HEAD_DIM
ROT_DIM = ATT_HEAD_DIM // 4
ROPE_THETA = 500000.0
RET_WIDTH = MIX_WIDTH - ATT_WIDTH
RET_HEAD_DIM = 256
RET_HEADS = RET_WIDTH // RET_HEAD_DIM
RET_THETA = 10000.0
D_FF = -(-8 * D_MODEL // (3 * 256)) * 256
SPLITS = [ATT_WIDTH, KV_WIDTH, KV_WIDTH, RET_WIDTH, RET_WIDTH, RET_WIDTH, RET_WIDTH]
SPLIT_IDX = [int(s) for s in np.cumsum(SPLITS)[:-1]]
IN_COLS = int(sum(SPLITS))
EPS = 1e-6
NEG = -1e30

kernel_name = "hymba_style_swa_retention_encoder"


def rms_norm(x, g):
    xf = x.astype(jnp.float32)
    y = xf * lax.rsqrt(jnp.mean(xf * xf, axis=-1, keepdims=True) + EPS)
    return (y * g.astype(jnp.float32)).astype(x.dtype)


def rope(x, pos, theta, rot_dim):
    half = rot_dim // 2
    inv = theta ** (-jnp.arange(half, dtype=jnp.float32) / half)
    ang = pos.astype(jnp.float32)[:, None] * inv[None, :]
    cos = jnp.cos(ang).astype(x.dtype)
    sin = jnp.sin(ang).astype(x.dtype)
    x1, x2, rest = x[..., :half], x[..., half:rot_dim], x[..., rot_dim:]
    return jnp.concatenate([x1 * cos - x2 * sin, x2 * cos + x1 * sin, rest], axis=-1)


def band_blocks(t):
    b_, h_, n_, d_ = t.shape
    nb = n_ // BLOCK
    tb = t.reshape(b_, h_, nb, BLOCK, d_)
    z = jnp.zeros_like(tb[:, :, :1])
    prev = jnp.concatenate([z, tb[:, :, :-1]], axis=2)
    nxt = jnp.concatenate([tb[:, :, 1:], z], axis=2)
    return jnp.concatenate([prev, tb, nxt], axis=3)


def windowed_sink_attention(q, k, v, sink):
    b_, hq, n_, dh = q.shape
    hkv = k.shape[1]
    g_ = hq // hkv
    nb = n_ // BLOCK
    qb = q.reshape(b_, hkv, g_, nb, BLOCK, dh)
    kb, vb = band_blocks(k), band_blocks(v)
    km, vm = k[:, :, PAD_FRONT:BLOCK], v[:, :, PAD_FRONT:BLOCK]
    qi = jnp.arange(nb)[:, None] * BLOCK + jnp.arange(BLOCK)[None, :]
    kj = (jnp.arange(nb)[:, None] - 1) * BLOCK + jnp.arange(3 * BLOCK)[None, :]
    band_ok = ((jnp.abs(qi[:, :, None] - kj[:, None, :]) <= WINDOW)
               & (kj[:, None, :] >= PAD_FRONT) & (kj[:, None, :] < n_))
    mj = PAD_FRONT + jnp.arange(N_META)
    meta_ok = jnp.abs(qi[:, :, None] - mj[None, None, :]) > WINDOW
    scale = dh ** -0.5
    s_band = jnp.einsum('bkgnqd,bknjd->bkgnqj', qb, kb).astype(jnp.float32) * scale
    s_meta = jnp.einsum('bkgnqd,bkmd->bkgnqm', qb, km).astype(jnp.float32) * scale
    s_sink = jnp.broadcast_to(sink.astype(jnp.float32).reshape(hkv, g_)[None, :, :, None, None, None],
                              s_band.shape[:-1] + (1,))
    s = jnp.concatenate([jnp.where(band_ok, s_band, NEG), jnp.where(meta_ok, s_meta, NEG), s_sink], axis=-1)
    p = jax.nn.softmax(s, axis=-1).astype(v.dtype)
    nk = 3 * BLOCK
    o = (jnp.einsum('bkgnqj,bknjd->bkgnqd', p[..., :nk], vb)
         + jnp.einsum('bkgnqm,bkmd->bkgnqd', p[..., nk:nk + N_META], vm))
    return o.reshape(b_, hq, n_, dh)


def retention_direction(q, k, v, log_gamma, include_diag):
    b_, h_, n_, dk = q.shape
    dv = v.shape[-1]
    nc = n_ // BLOCK
    qc = q.reshape(b_, h_, nc, BLOCK, dk)
    kc = k.reshape(b_, h_, nc, BLOCK, dk)
    vc = v.reshape(b_, h_, nc, BLOCK, dv)
    lg = log_gamma[:, None]
    idx = jnp.arange(BLOCK, dtype=jnp.float32)
    diff = idx[:, None] - idx[None, :]
    keep = (diff >= 0) if include_diag else (diff > 0)
    dmask = jnp.where(keep[None], jnp.exp(lg[:, :, None] * jnp.maximum(diff, 0.0)[None]), 0.0)
    scores = jnp.einsum('bhcid,bhcjd->bhcij', qc, kc) * dmask[None, :, None]
    intra = jnp.einsum('bhcij,bhcjv->bhciv', scores, vc)
    zeta = jnp.exp(lg * (BLOCK - 1 - idx)[None, :])
    kv_chunk = jnp.einsum('bhcjd,hj,bhcjv->bhcdv', kc, zeta, vc)
    decay_chunk = jnp.exp(log_gamma * BLOCK)[None, :, None, None]

    def step(state, kv_c):
        return state * decay_chunk + kv_c, state

    _, states = lax.scan(step, jnp.zeros((b_, h_, dk, dv), jnp.float32), jnp.moveaxis(kv_chunk, 2, 0))
    states = jnp.moveaxis(states, 0, 2)
    xi = jnp.exp(lg * (idx + 1.0)[None, :])
    cross = jnp.einsum('bhcid,hi,bhcdv->bhciv', qc, xi, states)
    return (intra + cross).reshape(b_, h_, n_, dv)


def bidirectional_retention(q, k, v, raw_fwd, raw_bwd):
    lg_f = -jnp.exp(raw_fwd.astype(jnp.float32))
    lg_b = -jnp.exp(raw_bwd.astype(jnp.float32))
    qf, kf, vf = q.astype(jnp.float32), k.astype(jnp.float32), v.astype(jnp.float32)
    fwd = retention_direction(qf, kf, vf, lg_f, True)
    bwd = retention_direction(qf[:, :, ::-1], kf[:, :, ::-1], vf[:, :, ::-1], lg_b, False)[:, :, ::-1]
    return fwd + bwd


def to_heads(t, n_heads, dh):
    b_, n_, _ = t.shape
    return t.reshape(b_, n_, n_heads, dh).transpose(0, 2, 1, 3)


def setup_inputs(seed: int = 0) -> dict:
    key = jax.random.key(seed)
    ks = jax.random.split(key, 16)
    f32 = jnp.float32

    def nrm(k, shape, scale):
        return jax.random.normal(k, shape, f32) * scale

    def gain(k, shape):
        return 1.0 + 0.02 * jax.random.normal(k, shape, f32)

    base_decay = np.log(-np.log(1.0 - 2.0 ** (-5.0 - np.arange(RET_HEADS)))).astype(np.float32)
    return {
        "x": nrm(ks[0], (BATCH, SEQ, D_MODEL), 1.0),
        "meta_tokens": nrm(ks[1], (N_META, D_MODEL), 1.0),
        "w_in": nrm(ks[2], (DEPTH, D_MODEL, IN_COLS), D_MODEL ** -0.5),
        "w_out": nrm(ks[3], (DEPTH, MIX_WIDTH, D_MODEL), MIX_WIDTH ** -0.5),
        "attn_sink": nrm(ks[4], (DEPTH, ATT_HEADS), 0.5),
        "ret_decay_fwd": jnp.asarray(base_decay)[None, :] + nrm(ks[5], (DEPTH, RET_HEADS), 0.01),
        "ret_decay_bwd": jnp.asarray(base_decay)[None, :] + nrm(ks[6], (DEPTH, RET_HEADS), 0.01),
        "ret_norm": gain(ks[7], (DEPTH, RET_WIDTH)),
        "norm_mix_pre": gain(ks[8], (DEPTH, D_MODEL)),
        "norm_mix_post": gain(ks[9], (DEPTH, D_MODEL)),
        "w_gate": nrm(ks[10], (DEPTH, D_MODEL, D_FF), D_MODEL ** -0.5),
        "w_up": nrm(ks[11], (DEPTH, D_MODEL, D_FF), D_MODEL ** -0.5),
        "w_down": nrm(ks[12], (DEPTH, D_FF, D_MODEL), D_FF ** -0.5),
        "norm_ffn_pre": gain(ks[13], (DEPTH, D_MODEL)),
        "norm_ffn_post": gain(ks[14], (DEPTH, D_MODEL)),
    }


def reference(x, meta_tokens, w_in, w_out, attn_sink, ret_decay_fwd, ret_decay_bwd, ret_norm,
              norm_mix_pre, norm_mix_post, w_gate, w_up, w_down, norm_ffn_pre, norm_ffn_post):
    b_ = x.shape[0]
    meta = jnp.broadcast_to(meta_tokens.astype(x.dtype)[None], (b_, N_META, D_MODEL))
    h = jnp.concatenate([meta, x], axis=1)
    n_pad = PAD_FRONT + h.shape[1]
    pos = jnp.arange(n_pad) - PAD_FRONT
    for l in range(DEPTH):
        u = rms_norm(h, norm_mix_pre[l])
        proj = jnp.einsum('bnd,dc->bnc', u, w_in[l])
        proj = jnp.pad(proj, ((0, 0), (PAD_FRONT, 0), (0, 0)))
        aq, ak, av, rq, rk, rv, rg = jnp.split(proj, SPLIT_IDX, axis=-1)
        aq = rope(to_heads(aq, ATT_HEADS, ATT_HEAD_DIM), pos, ROPE_THETA, ROT_DIM)
        ak = rope(to_heads(ak, ATT_KV_HEADS, ATT_HEAD_DIM), pos, ROPE_THETA, ROT_DIM)
        av = to_heads(av, ATT_KV_HEADS, ATT_HEAD_DIM)
        att = windowed_sink_attention(aq, ak, av, attn_sink[l])
        att = att.transpose(0, 2, 1, 3).reshape(b_, n_pad, ATT_WIDTH)
        rq = rope(to_heads(rq, RET_HEADS, RET_HEAD_DIM), pos, RET_THETA, RET_HEAD_DIM)
        rk = rope(to_heads(rk, RET_HEADS, RET_HEAD_DIM), pos, RET_THETA, RET_HEAD_DIM) * (RET_HEAD_DIM ** -0.5)
        rv = to_heads(rv, RET_HEADS, RET_HEAD_DIM)
        ret = bidirectional_retention(rq, rk, rv, ret_decay_fwd[l], ret_decay_bwd[l])
        ret = rms_norm(ret.transpose(0, 2, 1, 3), ret_norm[l].reshape(RET_HEADS, RET_HEAD_DIM))
        ret = ret.reshape(b_, n_pad, RET_WIDTH).astype(h.dtype)
        ret = jax.nn.silu(rg) * ret
        mixed = jnp.concatenate([att, ret], axis=-1)[:, PAD_FRONT:]
        mixed = jnp.einsum('bnc,cd->bnd', mixed, w_out[l])
        h = h + rms_norm(mixed, norm_mix_post[l])
        u = rms_norm(h, norm_ffn_pre[l])
        f = jax.nn.silu(jnp.einsum('bnd,df->bnf', u, w_gate[l])) * jnp.einsum('bnd,df->bnf', u, w_up[l])
        f = jnp.einsum('bnf,fd->bnd', f, w_down[l])
        h = h + rms_norm(f, norm_ffn_post[l])
    return h[:, N_META:]
```

```python
import os
from contextlib import ExitStack
import numpy as np
import ml_dtypes
import concourse.bass as bass
import concourse.mybir as mybir
from concourse.bass_utils import run_bass_kernel_spmd

F32 = mybir.dt.float32
BF16 = mybir.dt.bfloat16
ALU = mybir.AluOpType
AF = mybir.ActivationFunctionType

D = 2048
KC = 16
DFF = 5632
FC = 44
INC = 5632
N_META = 16
PADF = 112
EPS = 1e-6
ENGS = ("pe", "act", "dve", "pool", "sp")


class Sched:
    def __init__(self, nc, stack, n_dma_sems=32, same_engine_sync=True):
        self.nc = nc
        self.same_engine_sync = same_engine_sync
        self.esem = {e: stack.enter_context(nc.semaphore("s_" + e)) for e in ENGS}
        self.cnt = {e: 0 for e in ENGS}
        self.prog = {e: [] for e in ENGS}
        self.seen = {e: {} for e in ENGS}
        self.dsem = [stack.enter_context(nc.semaphore("d_%d" % i)) for i in range(n_dma_sems)]
        self.dcnt = [0] * n_dma_sems
        self.nsp = (n_dma_sems * 5) // 8
        self.drr = {"sp": 0, "pool": 0}
        self.res_w = {}
        self.res_r = {}
        self.semkey = {}
        for e in ENGS:
            self.semkey[id(self.esem[e])] = ("e", e)
        for i, s in enumerate(self.dsem):
            self.semkey[id(s)] = ("d", i)
        self.nops = 0
        self.dead = False
        self.csem = stack.enter_context(nc.semaphore("c_0"))
        self.semkey[id(self.csem)] = ("c", 0)
        self.ccnt = 0

    def _key(self, sem):
        return self.semkey[id(sem)]

    def _deps(self, reads, writes):
        deps = {}

        def add(ev):
            k = self._key(ev[0])
            if k not in deps or deps[k][1] < ev[1]:
                deps[k] = ev

        for r in reads:
            if r in self.res_w:
                add(self.res_w[r])
        for w in writes:
            if w in self.res_w:
                add(self.res_w[w])
            for ev in self.res_r.get(w, {}).values():
                add(ev)
        return deps

    def _emit_waits(self, eng, deps):
        for k, ev in deps.items():
            if k == ("e", eng):
                if eng in ("pe", "sp") or not self.same_engine_sync:
                    continue
            if self.seen[eng].get(k, 0) >= ev[1]:
                continue
            self.seen[eng][k] = ev[1]
            self.prog[eng].append(("wait", ev[0], ev[1]))

    def _record(self, ev, reads, writes):
        k = self._key(ev[0])
        for r in reads:
            self.res_r.setdefault(r, {})[k] = ev
        for w in writes:
            self.res_w[w] = ev
            self.res_r[w] = {}

    def op(self, eng, fn, reads=(), writes=()):
        if self.dead:
            return None
        deps = self._deps(reads, writes)
        self._emit_waits(eng, deps)
        self.cnt[eng] += 1
        ev = (self.esem[eng], self.cnt[eng])
        self.prog[eng].append(("op", fn, ev[0], 1))
        self._record(ev, reads, writes)
        self.nops += 1
        return ev

    def dma(self, fn, reads=(), writes=(), q="sp"):
        if self.dead:
            return None
        deps = self._deps(reads, writes)
        if q == "sp":
            i = self.drr["sp"]
            self.drr["sp"] = (i + 1) % self.nsp
        else:
            i = self.nsp + self.drr["pool"]
            self.drr["pool"] = (self.drr["pool"] + 1) % (len(self.dsem) - self.nsp)
        if self.dcnt[i] > 0:
            k = ("d", i)
            prev = (self.dsem[i], 16 * self.dcnt[i])
            if k not in deps or deps[k][1] < prev[1]:
                deps[k] = prev
        self._emit_waits(q, deps)
        self.dcnt[i] += 1
        ev = (self.dsem[i], 16 * self.dcnt[i])
        self.prog[q].append(("op", fn, ev[0], 16))
        self._record(ev, reads, writes)
        self.nops += 1
        return ev

    def coll(self, fn, reads=(), writes=()):
        if self.dead:
            return None
        deps = self._deps(reads, writes)
        self._emit_waits("pool", deps)
        self.ccnt += 1
        ev = (self.csem, self.ccnt)
        self.prog["pool"].append(("op", fn, self.csem, 1))
        self._record(ev, reads, writes)
        return ev

    def barrier(self, skip_coll=False):
        evs = {}
        if self.ccnt > 0 and not skip_coll:
            evs[("c", 0)] = (self.csem, self.ccnt)
        for e in ENGS:
            if self.cnt[e] > 0:
                evs[("e", e)] = (self.esem[e], self.cnt[e])
        for i, s in enumerate(self.dsem):
            if self.dcnt[i] > 0:
                evs[("d", i)] = (s, 16 * self.dcnt[i])
        for e in ENGS:
            d = {k: v for k, v in evs.items() if k != ("e", e)}
            self._emit_waits(e, d)

    def emit(self):
        nc = self.nc
        prog = self.prog

        def replay(engobj, items):
            for it in items:
                if it[0] == "wait":
                    engobj.wait_ge(it[1], it[2])
                else:
                    it[1](engobj).then_inc(it[2], it[3])

        with nc.Block() as block:
            @block.tensor
            def _(e):
                replay(e, prog["pe"])

            @block.scalar
            def _(e):
                replay(e, prog["act"])

            @block.vector
            def _(e):
                replay(e, prog["dve"])

            @block.gpsimd
            def _(e):
                replay(e, prog["pool"])

            @block.sync
            def _(e):
                replay(e, prog["sp"])
        self.prog = {e: [] for e in ENGS}


class Rot:
    def __init__(self, n):
        self.n = n
        self.i = 0

    def next(self):
        v = self.i
        self.i = (self.i + 1) % self.n
        return v


def token_groups(t0, t1, gmax=4):
    out = []
    t = t0
    while t < t1:
        n = min(gmax, t1 - t)
        out.append((t, n))
        t += n
    return out


class _Stop(Exception):
    pass


def build_program(L=4, NT=33, supers=None, debug=False):
    stop_after = os.environ.get("KSTOP", "")
    T = NT * 128
    if supers is None:
        supers = [(0, NT)]
    nc = bass.Bass("TRN2", target_bir_lowering=False)
    okind = "ExternalOutput" if debug else "Internal"

    def dram(name, shape, dt, kind="Internal"):
        return nc.dram_tensor(name, shape, dt, kind=kind).ap()

    h0T = dram("h0T", [KC, 128, T], F32, "ExternalInput")
    w_in = dram("w_in", [L, D, INC], F32, "ExternalInput")
    w_out = dram("w_out", [L, D, D], F32, "ExternalInput")
    w_gate = dram("w_gate", [L, D, DFF], F32, "ExternalInput")
    w_up = dram("w_up", [L, D, DFF], F32, "ExternalInput")
    w_down = dram("w_down", [L, DFF, D], F32, "ExternalInput")
    gains_d = dram("gains", [128, L * 4 * KC], F32, "ExternalInput")
    rn_d = dram("retnorm", [128, L * 8], F32, "ExternalInput")
    sd_d = dram("sinkdecay", [128, L * 16], F32, "ExternalInput")
    ropeR_d = dram("ropeR", [4, 128, T], F32, "ExternalInput")
    ropeA_d = dram("ropeA", [2, 32, T], F32, "ExternalInput")
    cbf_d = dram("cbf", [128, 128 * 2 + 32 + 512 * 5], F32, "ExternalInput")
    cf_d = dram("cf", [128, 128 * 6 + 8 + 128], F32, "ExternalInput")
    outT = dram("outT", [KC, 128, T], F32, "ExternalOutput")
    hX = dram("hX", [KC, 128, T], F32)
    hY = dram("hY", [KC, 128, T], F32)
    aqT = dram("aqT", [8, 128, T], BF16, okind)
    akT = dram("akT", [2, 128, T], BF16, okind)
    av2 = dram("av2", [2, NT, 128, 128], BF16, okind)
    rqT = dram("rqT", [8, 128, T], BF16, okind)
    rkT = dram("rkT", [8, 128, T], BF16, okind)
    rkM = dram("rkM", [NT, 128, 1024], BF16, okind)
    rvM = dram("rvM", [NT, 128, 1024], BF16, okind)
    rgM = dram("rgM", [NT, 128, 1024], BF16, okind)
    mixT = dram("mixT", [KC, 128, T], BF16, okind)
    fT_d = dram("fT_d", [FC, 128, T], BF16)
    uT_d = dram("uT_d", [KC, 128, T], BF16)
    u2T_d = dram("u2T_d", [KC, 128, T], BF16)
    XA = 12 * 128
    xs_att = dram("xs_att", [XA, 128], BF16)
    xr_att = dram("xr_att", [2 * XA, 128], BF16)
    XR = 8 * 128
    xs_ret = dram("xs_ret", [XR, 512], F32)
    xr_ret = dram("xr_ret", [2 * XR, 512], F32)
    RG = [[0, 1], [2, 3], [4, 5], [6, 7]]
    HALO = {0: 0, 1: 1, NT - 1: 2}

    def xblk(g, ti, kind):
        return ((g * 3 + ti) * 2 + kind) * 128

    with ExitStack() as st:
        S = Sched(nc, st)

        uid = [0]

        def sb(stack, name, shape, dt):
            uid[0] += 1
            return stack.enter_context(nc.sbuf_tensor("sb%d_%s" % (uid[0], name), shape, dt))

        def psum(stack, name, shape, dt):
            uid[0] += 1
            return stack.enter_context(nc.psum_tensor("ps%d_%s" % (uid[0], name), shape, dt))

        gains = sb(st, "gains", [128, L * 4 * KC], F32)
        rn = sb(st, "rn", [128, L * 8], F32)
        sdraw = sb(st, "sdraw", [128, L * 16], F32)
        sdexp = sb(st, "sdexp", [128, L * 16], F32)
        cbf = sb(st, "cbf", [128, 128 * 2 + 32 + 512 * 5], BF16)
        cf = sb(st, "cf", [128, 128 * 6 + 8 + 128], F32)
        epsc = sb(st, "epsc", [128, 1], F32)
        ones = cbf[:, 0:128]
        ident = cbf[:, 128:256]
        perm = cbf[:, 256:288]
        masks = {"meta": cbf[:, 288:800], "ge": cbf[:, 800:1312], "le": cbf[:, 1312:1824],
                 "xprev": cbf[:, 1824:2336], "xnext": cbf[:, 2336:2848]}
        P1 = cf[:, 0:128]
        MF = cf[:, 128:256]
        P2 = cf[:, 256:384]
        MB = cf[:, 384:512]
        ROW1 = cf[:, 512:640]
        ROW2 = cf[:, 640:768]
        COLS = cf[:, 768:776]
        TMASK = cf[:, 776:904]
        pb = [psum(st, "pb%d" % i, [128, 512], F32) for i in range(6)]
        pbf = [psum(st, "pbf%d" % i, [128, 1024], BF16) for i in range(2)]
        PB = [("pb", i) for i in range(6)]
        PBF = [("pbf", i) for i in range(2)]

        initst = ExitStack()
        cbf32 = sb(initst, "cbf32", [128, 128 * 2 + 32 + 512 * 5], F32)
        S.dma(lambda e: e.dma_start(out=gains[:], in_=gains_d), writes=["gains"])
        S.dma(lambda e: e.dma_start(out=rn[:], in_=rn_d), writes=["rn"])
        S.dma(lambda e: e.dma_start(out=sdraw[:], in_=sd_d), writes=["sdraw"])
        S.dma(lambda e: e.dma_start(out=cbf32[:], in_=cbf_d), writes=["cbf32"])
        S.dma(lambda e: e.dma_start(out=cf[:], in_=cf_d), writes=["cf"])
        S.op("pool", lambda e: e.tensor_copy(out=cbf[:], in_=cbf32[:]), reads=["cbf32"], writes=["cbf"])
        S.op("pool", lambda e: e.memset(epsc[:], EPS), writes=["epsc"])
        S.op("act", lambda e: e.activation(out=sdexp[:], in_=sdraw[:], func=AF.Exp), reads=["sdraw"], writes=["sdexp"])
        for l in range(L):
            S.op("dve", lambda e, l=l: e.tensor_scalar(out=sdexp[:, l * 16 + 8:l * 16 + 16], in0=sdexp[:, l * 16 + 8:l * 16 + 16],
                                                      scalar1=-1.0, scalar2=None, op0=ALU.mult),
                 reads=["sdexp"], writes=["sdexp"])
        S.barrier()
        S.emit()
        initst.close()

        def gcol(l, j, k):
            c = (l * 4 + j) * KC + k
            return gains[:, c:c + 1]

        def rstd_from_ss(ss_bank_ap, rstd_ap, inv_n, tag, bankres, extra_mask=None):
            S.op("act", lambda e: e.activation(out=rstd_ap, in_=ss_bank_ap, func=AF.Sqrt, scale=inv_n, bias=epsc[:, 0:1]),
                 reads=["epsc"], writes=[bankres, tag])
            S.op("dve", lambda e: e.reciprocal(out=rstd_ap, in_=rstd_ap), writes=[tag])
            if extra_mask is not None:
                S.op("dve", lambda e: e.tensor_tensor(out=rstd_ap[:, 0:128], in0=rstd_ap[:, 0:128], in1=extra_mask, op=ALU.mult), writes=[tag])

        def prenorm(ph, hsrc, c0, N, l, j, ut, ucol0, tagp):
            hb = ph["hb"]
            sqb = ph["sqb"]
            rstd = ph["rstd"]
            for q4 in range(4):
                S.dma(lambda e, q4=q4: e.dma_start(out=hb[:, q4 * 4:(q4 + 1) * 4, 0:N],
                                                    in_=hsrc[q4 * 4:(q4 + 1) * 4, :, c0:c0 + N].rearrange("k p n -> p k n")),
                      writes=[("hb", q4)])
            for k in range(KC):
                sl = k % 4
                S.op("act", lambda e, k=k, sl=sl: e.activation(out=sqb[:, sl, 0:N], in_=hb[:, k, 0:N], func=AF.Square),
                     reads=[("hb", k // 4)], writes=[("sqb", sl)])
                S.op("pe", lambda e, k=k, sl=sl: e.matmul(pb[5][:, 0:N], lhsT=ones, rhs=sqb[:, sl, 0:N],
                                                          start=(k == 0), stop=(k == KC - 1)),
                     reads=[("sqb", sl), "cbf"], writes=[PB[5]])
            rstd_from_ss(pb[5][:, 0:N], rstd[:, 0:N], 1.0 / D, "rstd", PB[5])
            for k in range(KC):
                eng = "dve"
                if eng == "dve":
                    S.op("dve", lambda e, k=k: e.scalar_tensor_tensor(out=ut[:, k, ucol0:ucol0 + N], in0=hb[:, k, 0:N],
                                                                     scalar=gcol(l, j, k), in1=rstd[:, 0:N],
                                                                     op0=ALU.mult, op1=ALU.mult),
                         reads=[("hb", k // 4), "rstd", "gains"], writes=[(tagp, k)])
                else:
                    ptmp = ph["ptmp"]
                    S.op("pool", lambda e, k=k: e.tensor_tensor(out=ptmp[:, 0:N], in0=hb[:, k, 0:N], in1=rstd[:, 0:N], op=ALU.mult),
                         reads=["rstd", ("hb", k // 4)], writes=["ptmp"])
                    S.op("pool", lambda e, k=k: e.tensor_scalar(out=ut[:, k, ucol0:ucol0 + N], in0=ptmp[:, 0:N],
                                                                scalar1=gcol(l, j, k), scalar2=None, op0=ALU.mult),
                         reads=["ptmp", "gains"], writes=[(tagp, k)])
            return hb

        def postnorm_residual(ph, hsrc, hdst, c0, N, l, j, reload_h):
            hb = ph["hb"]
            yb = ph["yb"]
            rstd = ph["rstd"]
            flush_ss()
            rstd_from_ss(pb[5][:, 0:N], rstd[:, 0:N], 1.0 / D, "rstd", PB[5],
                         extra_mask=(TMASK[:, 0:128] if c0 == 0 else None))
            if c0 == 0 and N > 128:
                pass
            if reload_h:
                for q4 in range(4):
                    S.dma(lambda e, q4=q4: e.dma_start(out=hb[:, q4 * 4:(q4 + 1) * 4, 0:N],
                                                        in_=hsrc[q4 * 4:(q4 + 1) * 4, :, c0:c0 + N].rearrange("k p n -> p k n")),
                          writes=[("hb", q4)])
            for k in range(KC):
                if "tmpf" in ph:
                    tmpf = ph["tmpf"]
                    S.op("dve", lambda e, k=k: e.tensor_tensor(out=tmpf[:, 0:N], in0=yb[:, k, 0:N], in1=rstd[:, 0:N], op=ALU.mult),
                         reads=["rstd", (ph.get("ybn", "yb"), k)], writes=["tmpf"])
                    S.op("dve", lambda e, k=k: e.tensor_tensor(out=hb[:, k, 0:N], in0=hb[:, k, 0:N], in1=tmpf[:, 0:N], op=ALU.add),
                         reads=["tmpf"], writes=[("hb", k // 4)])
                else:
                    S.op("dve", lambda e, k=k: e.tensor_tensor(out=yb[:, k, 0:N], in0=yb[:, k, 0:N], in1=rstd[:, 0:N], op=ALU.mult),
                         reads=["rstd"], writes=[("yb", k)])
                    S.op("dve", lambda e, k=k: e.tensor_tensor(out=hb[:, k, 0:N], in0=hb[:, k, 0:N], in1=yb[:, k, 0:N], op=ALU.add),
                         reads=[("yb", k)], writes=[("hb", k // 4)])
            for q4 in range(4):
                S.dma(lambda e, q4=q4: e.dma_start(out=hdst[q4 * 4:(q4 + 1) * 4, :, c0:c0 + N].rearrange("k p n -> p k n"),
                                                    in_=hb[:, q4 * 4:(q4 + 1) * 4, 0:N]),
                      reads=[("hb", q4)])

        def prenorm_hb(ph, N, l, j, dst, c0):
            hb, sqb, rstd, ustg = ph["hb"], ph["sqb"], ph["rstd"], ph["ustg"]
            for k in range(KC):
                sl = k % 4
                S.op("act", lambda e, k=k, sl=sl: e.activation(out=sqb[:, sl, 0:N], in_=hb[:, k, 0:N], func=AF.Square),
                     reads=[("hb", k // 4)], writes=[("sqb", sl)])
                S.op("pe", lambda e, k=k, sl=sl: e.matmul(pb[5][:, 0:N], lhsT=ones, rhs=sqb[:, sl, 0:N],
                                                          start=(k == 0), stop=(k == KC - 1)),
                     reads=[("sqb", sl), "cbf"], writes=[PB[5]])
            rstd_from_ss(pb[5][:, 0:N], rstd[:, 0:N], 1.0 / D, "rstd", PB[5])
            for q4 in range(4):
                us = ph["urot"].next()
                for kk in range(4):
                    k = q4 * 4 + kk
                    S.op("dve", lambda e, k=k, kk=kk, us=us: e.scalar_tensor_tensor(
                        out=ustg[us][:, kk, 0:N], in0=hb[:, k, 0:N], scalar=gcol(l, j, k), in1=rstd[:, 0:N],
                        op0=ALU.mult, op1=ALU.mult), reads=[("hb", q4), "rstd", "gains"], writes=[("ustg", us)])
                S.dma(lambda e, q4=q4, us=us: e.dma_start(
                    out=dst[q4 * 4:(q4 + 1) * 4, :, c0:c0 + N].rearrange("k p n -> p k n"), in_=ustg[us][:, :, 0:N]),
                    reads=[("ustg", us)])

        pend_ss = []

        def flush_ss(keep=0):
            while len(pend_ss) > keep:
                pend_ss.pop(0)()

        def evac_y(ph, bank, dchunk, N, l, j):
            yb = ph["yb"]
            sqb = ph["sqb"]
            sl = dchunk % 4
            S.op("act", lambda e: e.activation(out=yb[:, dchunk, 0:N], in_=pb[bank][:, 0:N], func=AF.Copy, scale=gcol(l, j, dchunk)),
                 reads=["gains"], writes=[PB[bank], (ph.get("ybn", "yb"), dchunk)])
            S.op("act", lambda e: e.activation(out=sqb[:, sl, 0:N], in_=pb[bank][:, 0:N], func=AF.Square),
                 writes=[PB[bank], ("sqb", sl)])
            pend_ss.append(lambda: S.op("pe", lambda e: e.matmul(pb[5][:, 0:N], lhsT=ones, rhs=sqb[:, sl, 0:N],
                                                               start=(dchunk == 0), stop=(dchunk == KC - 1)),
                                        reads=[("sqb", sl), "cbf"], writes=[PB[5]]))
            flush_ss(keep=2)

        def checkpoint(tag):
            if stop_after and stop_after == tag:
                S.dead = True

        try:
            for l in range(L if not stop_after else 1):
                h_in = h0T if l == 0 else hY
                h_mid = hX
                h_out = outT if l == L - 1 else hY

                for (t0, t1) in supers:
                    Ts = (t1 - t0) * 128
                    s0 = t0 * 128
                    groups = token_groups(t0, t1)
                    with ExitStack() as p12:
                        uT = sb(p12, "uT", [128, KC, Ts], BF16)
                        if l == 0:
                            with ExitStack() as p1:
                                ph = {"hb": sb(p1, "hb", [128, KC, 512], F32), "sqb": sb(p1, "sqb", [128, 4, 512], BF16),
                                      "rstd": sb(p1, "rstd", [128, 512], F32), "ptmp": sb(p1, "ptmp", [128, 512], F32)}
                                for (gt, gn) in groups:
                                    prenorm(ph, h_in, gt * 128, gn * 128, l, 0, uT, gt * 128 - s0, "uT")
                                S.barrier()
                                S.emit()
                                checkpoint("p1")
                        else:
                            for (gt, gn) in groups:
                                S.dma(lambda e, gt=gt, gn=gn: e.dma_start(
                                    out=uT[:, :, gt * 128 - s0:(gt + gn) * 128 - s0],
                                    in_=uT_d[:, :, gt * 128:(gt + gn) * 128].rearrange("k p n -> p k n")),
                                    writes=[("uTg", gt)])
                        with ExitStack() as p2:
                            wsl = [sb(p2, "wsl%d" % i, [128, KC, 512], BF16) for i in range(3)]
                            ropeR = sb(p2, "ropeR", [128, 4, Ts], F32)
                            ropeA = sb(p2, "ropeA", [32, 2, Ts], F32)
                            stg = [sb(p2, "stg%d" % i, [128, 2, 512], BF16) for i in range(3)]
                            stgT = [sb(p2, "stgT%d" % i, [128, 4, 512], BF16) for i in range(3)]
                            tmp = [sb(p2, "tmp%d" % i, [128, 512], F32) for i in range(4)]
                            wrot, srot, trot = Rot(3), Rot(3), Rot(3)
                            S.dma(lambda e: e.dma_start(out=ropeR[:], in_=ropeR_d[:, :, s0:s0 + Ts].rearrange("a p n -> p a n")), writes=["ropeR"])
                            S.dma(lambda e: e.dma_start(out=ropeA[:], in_=ropeA_d[:, :, s0:s0 + Ts].rearrange("a p n -> p a n")), writes=["ropeA"])
                            bankrot = Rot(2)
                            for si in [4, 5, 0, 1, 2, 3] + list(range(10, 14)) + [100, 101] + list(range(6, 10)) + [102, 103]:
                                ws = wrot.next()
                                if si < 100:
                                    wc0, wcn = si * 256, 256
                                else:
                                    wc0, wcn = (3584 if si < 102 else 4608) + (si % 2) * 512, 512
                                S.dma(lambda e, ws=ws, wc0=wc0, wcn=wcn: e.dma_start(
                                    out=wsl[ws][:, :, 0:wcn], in_=w_in[l].rearrange("(k p) c -> p k c", p=128)[:, :, wc0:wc0 + wcn]),
                                    writes=[("wsl", ws)], q="pool")
                                if si <= 4 or 6 <= si <= 13:
                                    for (gt, gn) in groups:
                                        N = gn * 128
                                        u0 = gt * 128 - s0
                                        bp = bankrot.next() * 2
                                        for c in range(2):
                                            for k in range(KC):
                                                S.op("pe", lambda e, c=c, k=k, bp=bp, ws=ws, u0=u0, N=N: e.matmul(
                                                    pb[bp + c][:, 0:N], lhsT=wsl[ws][:, k, c * 128:(c + 1) * 128],
                                                    rhs=uT[:, k, u0:u0 + N], start=(k == 0), stop=(k == KC - 1)),
                                                    reads=[("wsl", ws), ("uT", k), ("uTg", gt)], writes=[PB[bp + c]])
                                        sg = srot.next()
                                        if si <= 4:
                                            for c in range(2):
                                                S.op("act", lambda e, c=c, bp=bp, sg=sg, N=N: e.activation(
                                                    out=stg[sg][:, c, 0:N], in_=pb[bp + c][:, 0:N], func=AF.Copy),
                                                    writes=[PB[bp + c], ("stg", sg, c)])
                                                S.op("pe", lambda e, c=c, sg=sg, N=N: e.matmul(
                                                    pb[4][0:32, 0:N], lhsT=perm, rhs=stg[sg][:, c, 0:N], start=True, stop=True),
                                                    reads=[("stg", sg, c), "cbf"], writes=[PB[4]])
                                                S.op("dve", lambda e, c=c, bp=bp, N=N, u0=u0: e.tensor_tensor(
                                                    out=tmp[0][0:32, 0:N], in0=pb[bp + c][0:32, 0:N], in1=ropeA[:, 0, u0:u0 + N], op=ALU.mult),
                                                    reads=["ropeA"], writes=[PB[bp + c], ("tmp", 0)])
                                                S.op("dve", lambda e, N=N, u0=u0: e.tensor_tensor(
                                                    out=tmp[1][0:32, 0:N], in0=pb[4][0:32, 0:N], in1=ropeA[:, 1, u0:u0 + N], op=ALU.mult),
                                                    reads=["ropeA"], writes=[PB[4], ("tmp", 1)])
                                                S.op("dve", lambda e, c=c, sg=sg, N=N: e.tensor_tensor(
                                                    out=stg[sg][0:32, c, 0:N], in0=tmp[0][0:32, 0:N], in1=tmp[1][0:32, 0:N], op=ALU.add),
                                                    reads=[("tmp", 0), ("tmp", 1)], writes=[("stg", sg, c)])
                                            dst = aqT[2 * si:2 * si + 2] if si < 4 else akT
                                            S.dma(lambda e, sg=sg, N=N, dst=dst, gt=gt: e.dma_start(
                                                out=dst[:, :, gt * 128:gt * 128 + N].rearrange("h p n -> p h n"), in_=stg[sg][:, :, 0:N]),
                                                reads=[("stg", sg, 0), ("stg", sg, 1)])
                                            if si == 4:
                                                for jj in range(gn):
                                                    if (gt + jj) in HALO:
                                                        for g in range(2):
                                                            r0 = xblk(g, HALO[gt + jj], 0)
                                                            S.dma(lambda e, sg=sg, g=g, jj=jj, r0=r0: e.dma_start(
                                                                out=xs_att[r0:r0 + 128, :], in_=stg[sg][:, g, jj * 128:(jj + 1) * 128]),
                                                                reads=[("stg", sg, g)], writes=[("xs_att", r0)])
                                        else:
                                            isk = si >= 10
                                            ci, sn = (2, 3) if isk else (0, 1)
                                            hh = (si - 10) if isk else (si - 6)
                                            b0, b1 = bp, bp + 1
                                            S.op("dve", lambda e, N=N, u0=u0, b0=b0, ci=ci: e.tensor_tensor(
                                                out=tmp[0][:, 0:N], in0=pb[b0][:, 0:N], in1=ropeR[:, ci, u0:u0 + N], op=ALU.mult),
                                                reads=["ropeR"], writes=[PB[b0], ("tmp", 0)])
                                            S.op("dve", lambda e, N=N, u0=u0, b1=b1, sn=sn: e.tensor_tensor(
                                                out=tmp[1][:, 0:N], in0=pb[b1][:, 0:N], in1=ropeR[:, sn, u0:u0 + N], op=ALU.mult),
                                                reads=["ropeR"], writes=[PB[b1], ("tmp", 1)])
                                            S.op("dve", lambda e, N=N, sg=sg: e.tensor_tensor(
                                                out=stg[sg][:, 0, 0:N], in0=tmp[0][:, 0:N], in1=tmp[1][:, 0:N], op=ALU.subtract),
                                                reads=[("tmp", 0), ("tmp", 1)], writes=[("stg", sg, 0)])
                                            S.op("dve", lambda e, N=N, u0=u0, b1=b1, ci=ci: e.tensor_tensor(
                                                out=tmp[2][:, 0:N], in0=pb[b1][:, 0:N], in1=ropeR[:, ci, u0:u0 + N], op=ALU.mult),
                                                reads=["ropeR"], writes=[PB[b1], ("tmp", 2)])
                                            S.op("dve", lambda e, N=N, u0=u0, b0=b0, sn=sn: e.tensor_tensor(
                                                out=tmp[3][:, 0:N], in0=pb[b0][:, 0:N], in1=ropeR[:, sn, u0:u0 + N], op=ALU.mult),
                                                reads=["ropeR"], writes=[PB[b0], ("tmp", 3)])
                                            S.op("dve", lambda e, N=N, sg=sg: e.tensor_tensor(
                                                out=stg[sg][:, 1, 0:N], in0=tmp[2][:, 0:N], in1=tmp[3][:, 0:N], op=ALU.add),
                                                reads=[("tmp", 2), ("tmp", 3)], writes=[("stg", sg, 1)])
                                            dst = rkT if isk else rqT
                                            S.dma(lambda e, sg=sg, N=N, dst=dst, gt=gt, hh=hh: e.dma_start(
                                                out=dst[2 * hh:2 * hh + 2, :, gt * 128:gt * 128 + N].rearrange("h p n -> p h n"),
                                                in_=stg[sg][:, :, 0:N]),
                                                reads=[("stg", sg, 0), ("stg", sg, 1)])
                                            if isk:
                                                tg = trot.next()
                                                fb = tg % 2
                                                for jj in range(gn):
                                                    for c in range(2):
                                                        S.op("pe", lambda e, jj=jj, c=c, sg=sg, fb=fb: e.transpose(
                                                            out=pbf[fb][:, jj * 256 + c * 128:jj * 256 + (c + 1) * 128],
                                                            in_=stg[sg][:, c, jj * 128:(jj + 1) * 128], identity=ident),
                                                            reads=[("stg", sg, c), "cbf"], writes=[PBF[fb]])
                                                S.op("act", lambda e, tg=tg, fb=fb, gn=gn: e.activation(
                                                    out=stgT[tg][:, 0:gn, 0:256], in_=pbf[fb][:, 0:gn * 256].rearrange("p (t c) -> p t c", c=256),
                                                    func=AF.Copy), writes=[PBF[fb], ("stgT", tg)])
                                                S.dma(lambda e, tg=tg, gn=gn, gt=gt, hh=hh: e.dma_start(
                                                    out=rkM[gt:gt + gn, :, hh * 256:(hh + 1) * 256].rearrange("t p c -> p t c"),
                                                    in_=stgT[tg][:, 0:gn, 0:256]), reads=[("stgT", tg)])
                                else:
                                    for (gt, gn) in groups:
                                        tg = trot.next()
                                        for jj in range(gn):
                                            u0 = (gt + jj) * 128 - s0
                                            bk = bankrot.next() * 2 + (jj % 2)
                                            for k in range(KC):
                                                S.op("pe", lambda e, k=k, bk=bk, ws=ws, u0=u0, wcn=wcn: e.matmul(
                                                    pb[bk][:, 0:wcn], lhsT=uT[:, k, u0:u0 + 128], rhs=wsl[ws][:, k, 0:wcn],
                                                    start=(k == 0), stop=(k == KC - 1)),
                                                    reads=[("wsl", ws), ("uT", k), ("uTg", gt)], writes=[PB[bk]])
                                            fn = AF.Silu if si >= 102 else AF.Copy
                                            S.op("act", lambda e, bk=bk, tg=tg, jj=jj, fn=fn, wcn=wcn: e.activation(
                                                out=stgT[tg][:, jj, 0:wcn], in_=pb[bk][:, 0:wcn], func=fn),
                                                writes=[PB[bk], ("stgT", tg)])
                                        if si == 5:
                                            for g in range(2):
                                                S.dma(lambda e, tg=tg, gn=gn, gt=gt, g=g: e.dma_start(
                                                    out=av2[g, gt:gt + gn].rearrange("t p c -> p t c"),
                                                    in_=stgT[tg][:, 0:gn, g * 128:(g + 1) * 128]), reads=[("stgT", tg)])
                                                for jj in range(gn):
                                                    if (gt + jj) in HALO:
                                                        r0 = xblk(g, HALO[gt + jj], 1)
                                                        S.dma(lambda e, tg=tg, g=g, jj=jj, r0=r0: e.dma_start(
                                                            out=xs_att[r0:r0 + 128, :], in_=stgT[tg][:, jj, g * 128:(g + 1) * 128]),
                                                            reads=[("stgT", tg)], writes=[("xs_att", r0)])
                                        else:
                                            dst = rvM if si < 102 else rgM
                                            dc0 = (si % 2) * 512
                                            S.dma(lambda e, tg=tg, gn=gn, gt=gt, dc0=dc0, dst=dst: e.dma_start(
                                                out=dst[gt:gt + gn, :, dc0:dc0 + 512].rearrange("t p c -> p t c"),
                                                in_=stgT[tg][:, 0:gn, :]), reads=[("stgT", tg)])
                                    if si == 5:
                                        S.coll(lambda e: e.collective_compute("AllGather", ALU.bypass, replica_groups=RG,
                                                                              ins=[xs_att], outs=[xr_att]),
                                               reads=[("xs_att", r) for r in range(0, XA, 128)], writes=["xr_att"])
                            S.barrier()
                            S.emit()
                            checkpoint("p2")

                with ExitStack() as p2b:
                    kM4 = sb(p2b, "kM4", [128, 4, NT, 256], BF16)
                    v4 = sb(p2b, "v4", [128, 4, NT, 256], BF16)
                    zc4 = sb(p2b, "zc4", [128, 4, 4], F32)
                    St = sb(p2b, "St", [128, 8, 512], F32)
                    kzb = [sb(p2b, "kzb%d" % i, [128, 256], BF16) for i in range(6)]
                    kzrot, brot = Rot(6), Rot(6)
                    for hh in range(4):
                        lgf = sdexp[:, l * 16 + 8 + hh:l * 16 + 9 + hh]
                        lgb = sdexp[:, l * 16 + 12 + hh:l * 16 + 13 + hh]
                        for d_, (lg, col) in enumerate(((lgf, 0), (lgb, 1), (lgf, 2), (lgb, 2))):
                            S.op("act", lambda e, hh=hh, d_=d_, lg=lg, col=col: e.activation(
                                out=zc4[:, hh, d_:d_ + 1], in_=COLS[:, col:col + 1], func=AF.Exp, scale=lg),
                                reads=["cf", "sdexp"], writes=["zc4"])
                        for t0_ in range(0, NT, 8):
                            t1_ = min(NT, t0_ + 8)
                            S.dma(lambda e, hh=hh, t0_=t0_, t1_=t1_: e.dma_start(
                                out=kM4[:, hh, t0_:t1_, :], in_=rkM[t0_:t1_, :, hh * 256:(hh + 1) * 256].rearrange("t p c -> p t c")),
                                writes=[("kM4", hh)])
                            S.dma(lambda e, hh=hh, t0_=t0_, t1_=t1_: e.dma_start(
                                out=v4[:, hh, t0_:t1_, :], in_=rvM[t0_:t1_, :, hh * 256:(hh + 1) * 256].rearrange("t p c -> p t c")),
                                writes=[("v4", hh)])
                    S.op("pool", lambda e: e.memset(St[:], 0.0), writes=[("St", i) for i in range(8)])
                    for step in range(NT):
                        for hh in range(4):
                            for d_ in range(2):
                                tile_ = step if d_ == 0 else NT - 1 - step
                                if d_ == 1 and tile_ == 0:
                                    continue
                                kzs = kzrot.next()
                                bk = brot.next()
                                si_ = hh * 2 + d_
                                S.op("act", lambda e, hh=hh, d_=d_, tile_=tile_, kzs=kzs: e.activation(
                                    out=kzb[kzs][:], in_=kM4[:, hh, tile_, :], func=AF.Copy, scale=zc4[:, hh, d_:d_ + 1]),
                                    reads=[("kM4", hh), "zc4"], writes=[("kzb", kzs)])
                                for c in range(2):
                                    S.op("pe", lambda e, hh=hh, tile_=tile_, kzs=kzs, bk=bk, c=c: e.matmul(
                                        pb[bk][:, c * 256:(c + 1) * 256], lhsT=kzb[kzs][:, c * 128:(c + 1) * 128], rhs=v4[:, hh, tile_, :],
                                        start=True, stop=True), reads=[("kzb", kzs), ("v4", hh)], writes=[PB[bk]])
                                S.op("dve", lambda e, hh=hh, d_=d_, si_=si_, bk=bk: e.scalar_tensor_tensor(
                                    out=St[:, si_, :], in0=St[:, si_, :], scalar=zc4[:, hh, 2 + d_:3 + d_], in1=pb[bk][:, 0:512],
                                    op0=ALU.mult, op1=ALU.add), reads=["zc4"], writes=[PB[bk], ("St", si_)])
                    for si_ in range(8):
                        S.dma(lambda e, si_=si_: e.dma_start(out=xs_ret[si_ * 128:(si_ + 1) * 128, :], in_=St[:, si_, :]),
                              reads=[("St", si_)], writes=[("xs_ret", si_)])
                    S.coll(lambda e: e.collective_compute("AllGather", ALU.bypass, replica_groups=RG, ins=[xs_ret], outs=[xr_ret]),
                           reads=[("xs_ret", i) for i in range(8)], writes=["xr_ret"])
                    S.barrier(skip_coll=True)
                    S.emit()
                    checkpoint("p2b")

                quads = token_groups(0, NT)
                with ExitStack() as p3:
                    kTg = [sb(p3, "kTg%d" % g, [128, T], BF16) for g in range(2)]
                    vg = [sb(p3, "vg%d" % g, [128, NT, 128], BF16) for g in range(2)]
                    xk = [sb(p3, "xk%d" % g, [128, 3, 128], BF16) for g in range(2)]
                    xv = [sb(p3, "xv%d" % g, [128, 3, 128], BF16) for g in range(2)]
                    esrow = [sb(p3, "esrow%d" % g, [128, 512], F32) for g in range(2)]
                    q4 = [sb(p3, "q4_%d" % i, [128, 4, 512], BF16) for i in range(3)]
                    ost = [sb(p3, "ost%d" % i, [128, 4, 512], BF16) for i in range(2)]
                    E = [sb(p3, "E%d" % i, [128, 512], BF16) for i in range(10)]
                    den = [sb(p3, "den%d" % i, [128, 512], F32) for i in range(2)]
                    zero = sb(p3, "zero", [128, 512], F32)
                    S.op("pool", lambda e: e.memset(zero[:], 0.0), writes=["zero"])
                    erot, qrot, orot, drot, srot_, obrot = Rot(10), Rot(3), Rot(2), Rot(2), Rot(3), Rot(2)
                    for g in range(2):
                        S.dma(lambda e, g=g: e.dma_start(out=kTg[g][:], in_=akT[g]), writes=[("kTg", g)])
                        for t0_ in range(0, NT, 8):
                            t1_ = min(NT, t0_ + 8)
                            S.dma(lambda e, g=g, t0_=t0_, t1_=t1_: e.dma_start(out=vg[g][:, t0_:t1_, :],
                                                                               in_=av2[g, t0_:t1_].rearrange("t p c -> p t c")),
                                  writes=[("vg", g)])
                        for j_, (slot, ti) in enumerate(((0, 0), (0, 2), (1, 1))):
                            rk_ = slot * XA + xblk(g, ti, 0)
                            rv_ = slot * XA + xblk(g, ti, 1)
                            S.dma(lambda e, g=g, j_=j_, rk_=rk_: e.dma_start(out=xk[g][:, j_, :], in_=xr_att[rk_:rk_ + 128, :]),
                                  reads=["xr_att"], writes=[("xk", g)])
                            S.dma(lambda e, g=g, j_=j_, rv_=rv_: e.dma_start(out=xv[g][:, j_, :], in_=xr_att[rv_:rv_ + 128, :]),
                                  reads=["xr_att"], writes=[("xv", g)])
                        for hh in range(4):
                            ci = l * 16 + 4 * g + hh
                            S.op("dve", lambda e, g=g, hh=hh, ci=ci: e.tensor_scalar(out=esrow[g][:, hh * 128:(hh + 1) * 128], in0=zero[:, 0:128],
                                                                                     scalar1=sdexp[:, ci:ci + 1], scalar2=None, op0=ALU.add),
                                 reads=["zero", "sdexp"], writes=[("esrow", g)])
                    seq = [(g, n) for g in range(2) for n in range(NT)]
                    info = {}
                    qslot = {}

                    def att_stage1(g, n):
                        qt = (n // 4) * 4
                        qn = min(4, NT - qt)
                        jj = n - qt
                        if (g, qt) not in qslot:
                            qs = qrot.next()
                            qslot[(g, qt)] = qs
                            S.dma(lambda e: e.dma_start(
                                out=q4[qs][:, :, 0:qn * 128], in_=aqT[4 * g:4 * g + 4, :, qt * 128:(qt + qn) * 128].rearrange("h p n -> p h n")),
                                writes=[("q4", qs)])
                        qs = qslot[(g, qt)]
                        if n == 0:
                            kbl = [("X", 0, "meta"), ("L", 1, "le")]
                        elif n == 1:
                            kbl = [("X", 0, "meta"), ("X", 1, "xprev"), ("L", 1, None), ("L", 2, "le")]
                        elif n == NT - 1:
                            kbl = [("X", 0, "meta"), ("L", n - 1, "ge"), ("L", n, None), ("X", 2, "xnext")]
                        else:
                            kbl = [("X", 0, "meta"), ("L", n - 1, "ge"), ("L", n, None), ("L", n + 1, "le")]
                        if n == 1 and NT == 2:
                            kbl = [("X", 0, "meta"), ("X", 1, "xprev"), ("L", 1, None), ("X", 2, "xnext")]
                        lst = []
                        for (src, kb, m) in kbl:
                            bs = srot_.next()
                            es = erot.next()
                            if src == "L":
                                kap = kTg[g][:, kb * 128:(kb + 1) * 128]
                                vap = vg[g][:, kb, :]
                                kres, vres = ("kTg", g), ("vg", g)
                            else:
                                kap = xk[g][:, kb, :]
                                vap = xv[g][:, kb, :]
                                kres, vres = ("xk", g), ("xv", g)
                            S.op("pe", lambda e, bs=bs, kap=kap, qs=qs, jj=jj: e.matmul(
                                pb[bs][:, 0:512].rearrange("p (h n) -> p h n", h=4), lhsT=kap,
                                rhs=q4[qs][:, :, jj * 128:(jj + 1) * 128], start=True, stop=True),
                                reads=[kres, ("q4", qs)], writes=[PB[bs]])
                            S.op("act", lambda e, bs=bs, es=es: e.activation(out=E[es][:], in_=pb[bs][:, 0:512], func=AF.Exp,
                                                                             scale=float(128 ** -0.5)),
                                 writes=[PB[bs], ("E", es)])
                            if m is not None:
                                S.op("dve", lambda e, es=es, m=m: e.tensor_tensor(out=E[es][:], in0=E[es][:], in1=masks[m], op=ALU.mult),
                                     reads=["cbf"], writes=[("E", es)])
                            lst.append((es, vap, vres))
                        info[(g, n)] = lst

                    def att_stage2(g, n):
                        qt = (n // 4) * 4
                        qn = min(4, NT - qt)
                        jj = n - qt
                        lst = info.pop((g, n))
                        bo = 3 + obrot.next()
                        bd = 5
                        if jj == 0:
                            info[("os", g, qt)] = orot.next()
                        os_ = info[("os", g, qt)]
                        for idx, (es, vap, vres) in enumerate(lst):
                            last = idx == len(lst) - 1
                            S.op("pe", lambda e, bo=bo, vap=vap, es=es, idx=idx, last=last: e.matmul(
                                pb[bo][:, 0:512], lhsT=vap, rhs=E[es][:], start=(idx == 0), stop=last),
                                reads=[vres, ("E", es)], writes=[PB[bo]])
                            S.op("pe", lambda e, bd=bd, es=es, idx=idx, last=last: e.matmul(
                                pb[bd][:, 0:512], lhsT=ones, rhs=E[es][:], start=(idx == 0), stop=last),
                                reads=["cbf", ("E", es)], writes=[PB[bd]])
                        ds = drot.next()
                        S.op("dve", lambda e, ds=ds, bd=bd: e.tensor_tensor(out=den[ds][:], in0=pb[bd][:, 0:512], in1=esrow[g][:], op=ALU.add),
                             reads=[("esrow", g)], writes=[PB[bd], ("den", ds)])
                        S.op("dve", lambda e, ds=ds: e.reciprocal(out=den[ds][:], in_=den[ds][:]), writes=[("den", ds)])
                        S.op("dve", lambda e, ds=ds, bo=bo, os_=os_, jj=jj: e.tensor_tensor(
                            out=ost[os_][:, :, jj * 128:(jj + 1) * 128], in0=pb[bo][:, 0:512].rearrange("p (h n) -> p h n", h=4),
                            in1=den[ds][:].rearrange("p (h n) -> p h n", h=4), op=ALU.mult),
                            reads=[("den", ds)], writes=[PB[bo], ("ost", os_)])
                        if jj == qn - 1:
                            S.dma(lambda e, os_=os_: e.dma_start(
                                out=mixT[4 * g:4 * g + 4, :, qt * 128:(qt + qn) * 128].rearrange("h p n -> p h n"), in_=ost[os_][:, :, 0:qn * 128]),
                                reads=[("ost", os_)])

                    for i_ in range(len(seq) + 1):
                        if i_ < len(seq):
                            att_stage1(*seq[i_])
                        if i_ >= 1:
                            att_stage2(*seq[i_ - 1])
                    S.barrier(skip_coll=True)
                    S.emit()
                    checkpoint("p3a")

                pwo = ExitStack()
                wo = sb(pwo, "wo", [128, KC, D], BF16)
                for si in range(8):
                    S.dma(lambda e, si=si: e.dma_start(out=wo[:, :, si * 256:(si + 1) * 256],
                                                        in_=w_out[l].rearrange("(k p) c -> p k c", p=128)[:, :, si * 256:(si + 1) * 256]),
                          writes=[("wo", si)], q="pool")
                with ExitStack() as p3:
                    kTh = [sb(p3, "kTh%d" % i, [128, 2, T], BF16) for i in range(2)]
                    kMh = [sb(p3, "kMh%d" % i, [128, NT, 256], BF16) for i in range(2)]
                    vh = [sb(p3, "vh%d" % i, [128, NT, 256], BF16) for i in range(2)]
                    Rst = [sb(p3, "Rst%d" % i, [128, NT, 512], BF16) for i in range(2)]
                    DT = [sb(p3, "DT%d" % i, [128, 128], F32) for i in range(2)]
                    xi = [sb(p3, "xi%d" % i, [128, 2, 2, 128], BF16) for i in range(2)]
                    zc = [sb(p3, "zc%d" % i, [128, 4], F32) for i in range(2)]
                    Rb = [sb(p3, "Rb%d" % i, [128, 512], F32) for i in range(2)]
                    Sx = [sb(p3, "Sx%d" % i, [128, 512], F32) for i in range(2)]
                    tA = sb(p3, "tA", [128, 128], F32)
                    qq = [sb(p3, "qq%d" % i, [128, 2, 512], BF16) for i in range(3)]
                    gq = [sb(p3, "gq%d" % i, [128, 4, 256], BF16) for i in range(3)]
                    ost = [sb(p3, "rost%d" % i, [128, 2, 512], BF16) for i in range(2)]
                    Sf = sb(p3, "Sf", [128, 512], F32)
                    Sbf = [sb(p3, "Sbf%d" % i, [128, 512], BF16) for i in range(3)]
                    kz = [sb(p3, "kz%d" % i, [128, 256], BF16) for i in range(4)]
                    sm = [sb(p3, "sm%d" % i, [128, 128], BF16) for i in range(3)]
                    qf = [sb(p3, "qf%d" % i, [128, 2, 2, 128], BF16) for i in range(3)]
                    yv = [sb(p3, "yv%d" % i, [128, 256], BF16) for i in range(3)]
                    junk = sb(p3, "junk", [128, 256], F32)
                    ssr = [sb(p3, "ssr%d" % i, [128, 1], F32) for i in range(2)]
                    kzrot = Rot(4)
                    kvrot = Rot(2)

                    def head_setup(hh, p):
                        lgf = sdexp[:, l * 16 + 8 + hh:l * 16 + 9 + hh]
                        lgb = sdexp[:, l * 16 + 12 + hh:l * 16 + 13 + hh]
                        S.op("act", lambda e: e.activation(out=DT[p][:], in_=P1, func=AF.Exp, scale=lgf), reads=["cf", "sdexp"], writes=[("DT", p)])
                        S.op("dve", lambda e: e.tensor_tensor(out=DT[p][:], in0=DT[p][:], in1=MF, op=ALU.mult), reads=["cf"], writes=[("DT", p)])
                        S.op("act", lambda e: e.activation(out=tA[:], in_=P2, func=AF.Exp, scale=lgb), reads=["cf", "sdexp"], writes=["tA"])
                        S.op("dve", lambda e: e.tensor_tensor(out=tA[:], in0=tA[:], in1=MB, op=ALU.mult), reads=["cf"], writes=["tA"])
                        S.op("dve", lambda e: e.tensor_tensor(out=DT[p][:], in0=DT[p][:], in1=tA[:], op=ALU.add), reads=["tA"], writes=[("DT", p)])
                        for c in range(2):
                            S.op("act", lambda e, c=c: e.activation(out=xi[p][:, 0, c, :], in_=ROW1, func=AF.Exp, scale=lgf),
                                 reads=["cf", "sdexp"], writes=[("xi", p)])
                            S.op("act", lambda e, c=c: e.activation(out=xi[p][:, 1, c, :], in_=ROW2, func=AF.Exp, scale=lgb),
                                 reads=["cf", "sdexp"], writes=[("xi", p)])
                        for d_, (lg, col) in enumerate(((lgf, 0), (lgb, 1), (lgf, 2), (lgb, 2))):
                            S.op("act", lambda e, d_=d_, lg=lg, col=col: e.activation(
                                out=zc[p][:, d_:d_ + 1], in_=COLS[:, col:col + 1], func=AF.Exp, scale=lg),
                                reads=["cf", "sdexp"], writes=[("zc", p)])
                        S.dma(lambda e: e.dma_start(out=kTh[p][:], in_=rkT[2 * hh:2 * hh + 2].rearrange("c p n -> p c n")), writes=[("kTh", p)])
                        for t0_ in range(0, NT, 8):
                            t1_ = min(NT, t0_ + 8)
                            S.dma(lambda e, t0_=t0_, t1_=t1_: e.dma_start(
                                out=kMh[p][:, t0_:t1_, :], in_=rkM[t0_:t1_, :, hh * 256:(hh + 1) * 256].rearrange("t p c -> p t c")),
                                writes=[("kMh", p)])
                            S.dma(lambda e, t0_=t0_, t1_=t1_: e.dma_start(
                                out=vh[p][:, t0_:t1_, :], in_=rvM[t0_:t1_, :, hh * 256:(hh + 1) * 256].rearrange("t p c -> p t c")),
                                writes=[("vh", p)])
                        rb_ = XR + (hh * 2 + 1) * 128
                        sx_ = (hh * 2) * 128
                        S.dma(lambda e: e.dma_start(out=Rb[p][:], in_=xr_ret[rb_:rb_ + 128, :]), reads=["xr_ret"], writes=[("Rb", p)])
                        S.dma(lambda e: e.dma_start(out=Sx[p][:], in_=xr_ret[sx_:sx_ + 128, :]), reads=["xr_ret"], writes=[("Sx", p)])
                        S.op("dve", lambda e: e.tensor_scalar(out=Rb[p][:], in0=Rb[p][:], scalar1=COLS[:, 3:4], scalar2=None, op0=ALU.mult),
                             reads=["cf"], writes=[("Rb", p)])

                    def prepass_step(p, n):
                        kzs = kzrot.next()
                        bk = 2 + kvrot.next()
                        S.op("act", lambda e: e.activation(out=Rst[p][:, n, :], in_=Rb[p][:], func=AF.Copy), reads=[("Rb", p)], writes=[("Rst", p, n)])
                        S.op("act", lambda e: e.activation(out=kz[kzs][:], in_=kMh[p][:, n, :], func=AF.Copy, scale=zc[p][:, 1:2]),
                             reads=[("kMh", p), ("zc", p)], writes=[("kz", kzs)])
                        for c in range(2):
                            S.op("pe", lambda e, c=c: e.matmul(
                                pb[bk][:, c * 256:(c + 1) * 256], lhsT=kz[kzs][:, c * 128:(c + 1) * 128], rhs=vh[p][:, n, :],
                                start=True, stop=True), reads=[("kz", kzs), ("vh", p)], writes=[PB[bk]])
                        S.op("dve", lambda e: e.scalar_tensor_tensor(out=Rb[p][:], in0=Rb[p][:], scalar=zc[p][:, 3:4], in1=pb[bk][:, 0:512],
                                                                    op0=ALU.mult, op1=ALU.add),
                             reads=[("zc", p)], writes=[PB[bk], ("Rb", p)])

                    def quad_of(n):
                        qt = (n // 4) * 4
                        return qt, min(4, NT - qt), n - qt

                    def ret_A(hh, p, n, st_):
                        qt, qn, jj = quad_of(n)
                        if qt not in st_["q"]:
                            qs = (qt // 4) % 3
                            st_["q"][qt] = qs
                            S.dma(lambda e: e.dma_start(
                                out=qq[qs][:, :, 0:qn * 128], in_=rqT[2 * hh:2 * hh + 2, :, qt * 128:(qt + qn) * 128].rearrange("c p n -> p c n")),
                                writes=[("qq", qs)])
                        qs = st_["q"][qt]
                        s3 = n % 3
                        bsc = 4 + (n % 2)
                        bk = 2 + kvrot.next()
                        kzs = kzrot.next()
                        for c in range(2):
                            S.op("pe", lambda e, c=c: e.matmul(
                                pb[bsc][:, 0:128], lhsT=kTh[p][:, c, n * 128:(n + 1) * 128], rhs=qq[qs][:, c, jj * 128:(jj + 1) * 128],
                                start=(c == 0), stop=(c == 1)), reads=[("kTh", p), ("qq", qs)], writes=[PB[bsc]])
                        S.op("dve", lambda e: e.tensor_tensor(out=sm[s3][:], in0=pb[bsc][:, 0:128], in1=DT[p][:], op=ALU.mult),
                             reads=[("DT", p)], writes=[PB[bsc], ("sm", s3)])
                        for d_ in range(2):
                            S.op("dve", lambda e, d_=d_: e.tensor_tensor(
                                out=qf[s3][:, d_, :, :], in0=qq[qs][:, :, jj * 128:(jj + 1) * 128], in1=xi[p][:, d_, :, :], op=ALU.mult),
                                reads=[("qq", qs), ("xi", p)], writes=[("qf", s3)])
                        S.op("act", lambda e: e.activation(out=kz[kzs][:], in_=kMh[p][:, n, :], func=AF.Copy, scale=zc[p][:, 0:1]),
                             reads=[("kMh", p), ("zc", p)], writes=[("kz", kzs)])
                        for c in range(2):
                            S.op("pe", lambda e, c=c: e.matmul(
                                pb[bk][:, c * 256:(c + 1) * 256], lhsT=kz[kzs][:, c * 128:(c + 1) * 128], rhs=vh[p][:, n, :],
                                start=True, stop=True), reads=[("kz", kzs), ("vh", p)], writes=[PB[bk]])
                        S.op("dve", lambda e: e.scalar_tensor_tensor(out=Sf[:], in0=Sf[:], scalar=zc[p][:, 2:3], in1=pb[bk][:, 0:512],
                                                                    op0=ALU.mult, op1=ALU.add),
                             reads=[("zc", p)], writes=[PB[bk], "Sf"])
                        if n == 0:
                            S.op("dve", lambda e: e.scalar_tensor_tensor(out=Sf[:], in0=Sx[p][:], scalar=COLS[:, 4:5], in1=Sf[:],
                                                                        op0=ALU.mult, op1=ALU.add),
                                 reads=[("Sx", p), "cf"], writes=["Sf"])
                        nx = (n + 1) % 3
                        S.op("act", lambda e: e.activation(out=Sbf[nx][:], in_=Sf[:], func=AF.Copy), reads=["Sf"], writes=[("Sbf", nx)])

                    def ret_B(hh, p, n, st_):
                        qt, qn, jj = quad_of(n)
                        if qt not in st_["g"]:
                            gs = (qt // 4) % 3
                            st_["g"][qt] = gs
                            S.dma(lambda e: e.dma_start(
                                out=gq[gs][:, 0:qn, :], in_=rgM[qt:qt + qn, :, hh * 256:(hh + 1) * 256].rearrange("t p c -> p t c")),
                                writes=[("gq", gs)])
                        gs = st_["g"][qt]
                        s3 = n % 3
                        bo = n % 2
                        S.op("pe", lambda e: e.matmul(pb[bo][:, 0:256], lhsT=sm[s3][:], rhs=vh[p][:, n, :], start=True, stop=False),
                             reads=[("sm", s3), ("vh", p)], writes=[PB[bo]])
                        for c in range(2):
                            S.op("pe", lambda e, c=c: e.matmul(
                                pb[bo][:, 0:256], lhsT=qf[s3][:, 0, c, :], rhs=Sbf[s3][:, c * 256:(c + 1) * 256], start=False, stop=False),
                                reads=[("qf", s3), ("Sbf", s3)], writes=[PB[bo]])
                        for c in range(2):
                            S.op("pe", lambda e, c=c: e.matmul(
                                pb[bo][:, 0:256], lhsT=qf[s3][:, 1, c, :], rhs=Rst[p][:, n, c * 256:(c + 1) * 256], start=False, stop=(c == 1)),
                                reads=[("qf", s3), ("Rst", p, n)], writes=[PB[bo]])
                        sr = ssr[n % 2]
                        srn = ("ssr", n % 2)
                        S.op("act", lambda e: e.activation(out=junk[:], in_=pb[bo][:, 0:256], func=AF.Square, accum_out=sr[:, 0:1]),
                             writes=[PB[bo], "junk", srn])
                        S.op("act", lambda e: e.activation(out=sr[:, 0:1], in_=sr[:, 0:1], func=AF.Sqrt, scale=1.0 / 256, bias=epsc[:, 0:1]),
                             reads=["epsc"], writes=[srn])
                        S.op("dve", lambda e: e.reciprocal(out=sr[:, 0:1], in_=sr[:, 0:1]), writes=[srn])
                        S.op("dve", lambda e: e.scalar_tensor_tensor(
                            out=yv[s3][:], in0=pb[bo][:, 0:256], scalar=sr[:, 0:1], in1=gq[gs][:, jj, :], op0=ALU.mult, op1=ALU.mult),
                            reads=[srn, ("gq", gs)], writes=[PB[bo], ("yv", s3)])

                    def ret_C(hh, p, n, st_):
                        qt, qn, jj = quad_of(n)
                        if qt not in st_["o"]:
                            st_["o"][qt] = (qt // 4) % 2
                        os_ = st_["o"][qt]
                        s3 = n % 3
                        fb = n % 2
                        for c in range(2):
                            S.op("pe", lambda e, c=c: e.transpose(out=pbf[fb][:, c * 128:(c + 1) * 128],
                                                                  in_=yv[s3][:, c * 128:(c + 1) * 128], identity=ident),
                                 reads=[("yv", s3), "cbf"], writes=[PBF[fb]])
                        for c in range(2):
                            rc = l * 8 + 2 * hh + c
                            S.op("act", lambda e, c=c, rc=rc: e.activation(
                                out=ost[os_][:, c, jj * 128:(jj + 1) * 128], in_=pbf[fb][:, c * 128:(c + 1) * 128], func=AF.Copy,
                                scale=rn[:, rc:rc + 1]), reads=["rn"], writes=[PBF[fb], ("rost", os_)])
                        if jj == qn - 1:
                            S.dma(lambda e: e.dma_start(
                                out=mixT[8 + 2 * hh:8 + 2 * hh + 2, :, qt * 128:(qt + qn) * 128].rearrange("c p n -> p c n"),
                                in_=ost[os_][:, :, 0:qn * 128]), reads=[("rost", os_)])

                    head_setup(0, 0)
                    for n in range(NT - 1, -1, -1):
                        prepass_step(0, n)
                    for hh in range(4):
                        p = hh % 2
                        if hh + 1 < 4:
                            head_setup(hh + 1, 1 - p)
                        S.op("dve", lambda e: e.memset(Sf[:], 0.0), writes=["Sf"])
                        S.op("dve", lambda e: e.memset(Sbf[0][:], 0.0), writes=[("Sbf", 0)])
                        st_ = {"q": {}, "g": {}, "o": {}}
                        for i_ in range(NT + 2):
                            if i_ < NT:
                                ret_A(hh, p, i_, st_)
                            if 1 <= i_ <= NT:
                                ret_B(hh, p, i_ - 1, st_)
                            if i_ >= 2:
                                ret_C(hh, p, i_ - 2, st_)
                            if hh + 1 < 4 and i_ < NT:
                                prepass_step(1 - p, NT - 1 - i_)
                    S.barrier()
                    S.emit()
                    checkpoint("p3r")

                for (t0, t1) in supers:
                    groups = token_groups(t0, t1)
                    Ts5 = (t1 - t0) * 128
                    with ExitStack() as p4:
                        mg = [sb(p4, "mg%d" % i, [128, KC, 512], BF16) for i in range(2)]
                        yb2 = [sb(p4, "yb%d" % i, [128, KC, 512], BF16) for i in range(2)]
                        ph = {"hb": sb(p4, "hb", [128, KC, 512], F32), "yb": yb2[0], "ybn": "yb0",
                              "sqb": sb(p4, "sqb", [128, 4, 512], BF16), "rstd": sb(p4, "rstd", [128, 512], F32),
                              "tmpf": sb(p4, "tmpf", [128, 512], F32),
                              "ustg": [sb(p4, "ustg%d" % i, [128, 4, 512], BF16) for i in range(2)], "urot": Rot(2)}
                        mrot, brot = Rot(2), Rot(4)
                        rstdb = [ph["rstd"], sb(p4, "rstdB", [128, 512], F32)]
                        rstd2 = sb(p4, "rstd2", [128, 512], F32)
                        sqb2 = sb(p4, "sqb2", [128, 4, 512], BF16)
                        pending_tail = [None]
                        for gi, (gt, gn) in enumerate(groups):
                            N = gn * 128
                            c0 = gt * 128
                            ms = mrot.next()
                            ph["yb"] = yb2[ms]
                            ph["ybn"] = "yb%d" % ms
                            S.dma(lambda e, ms=ms, c0=c0, N=N: e.dma_start(out=mg[ms][:, :, 0:N], in_=mixT[:, :, c0:c0 + N].rearrange("k p n -> p k n")),
                                  writes=[("mg", ms)])
                            for dch in range(KC):
                                bk = brot.next()
                                for k in range(KC):
                                    S.op("pe", lambda e, bk=bk, k=k, dch=dch, ms=ms, N=N: e.matmul(
                                        pb[bk][:, 0:N], lhsT=wo[:, k, dch * 128:(dch + 1) * 128], rhs=mg[ms][:, k, 0:N],
                                        start=(k == 0), stop=(k == KC - 1)), reads=[("wo", dch // 2), ("mg", ms)], writes=[PB[bk]])
                                evac_y(ph, bk, dch, N, l, 1)
                                if dch == 3 and pending_tail[0] is not None:
                                    pending_tail[0]()
                                    pending_tail[0] = None
                            rs = gi % 2
                            flush_ss()
                            rstd_from_ss(pb[5][:, 0:N], rstdb[rs][:, 0:N], 1.0 / D, ("rstd", rs), PB[5],
                                         extra_mask=(TMASK[:, 0:128] if c0 == 0 else None))

                            def tail(c0=c0, N=N, yb=ph["yb"], ybn=ph["ybn"], rs=rs):
                                hb = ph["hb"]
                                tmpf = ph["tmpf"]
                                sqb = ph["sqb"]
                                ustg = ph["ustg"]
                                rs_ap = rstdb[rs]
                                for q4 in range(4):
                                    S.dma(lambda e, q4=q4: e.dma_start(out=hb[:, q4 * 4:(q4 + 1) * 4, 0:N],
                                                                        in_=h_in[q4 * 4:(q4 + 1) * 4, :, c0:c0 + N].rearrange("k p n -> p k n")),
                                          writes=[("hb", q4)])
                                for k in range(KC):
                                    S.op("dve", lambda e, k=k: e.tensor_tensor(out=tmpf[:, 0:N], in0=yb[:, k, 0:N], in1=rs_ap[:, 0:N], op=ALU.mult),
                                         reads=[("rstd", rs), (ybn, k)], writes=["tmpf"])
                                    S.op("dve", lambda e, k=k: e.tensor_tensor(out=hb[:, k, 0:N], in0=hb[:, k, 0:N], in1=tmpf[:, 0:N], op=ALU.add),
                                         reads=["tmpf"], writes=[("hb", k // 4)])
                                for q4 in range(4):
                                    S.dma(lambda e, q4=q4: e.dma_start(out=h_mid[q4 * 4:(q4 + 1) * 4, :, c0:c0 + N].rearrange("k p n -> p k n"),
                                                                        in_=hb[:, q4 * 4:(q4 + 1) * 4, 0:N]),
                                          reads=[("hb", q4)])
                                for k in range(KC):
                                    sl = k % 4
                                    S.op("act", lambda e, k=k, sl=sl: e.activation(out=sqb2[:, sl, 0:N], in_=hb[:, k, 0:N], func=AF.Square),
                                         reads=[("hb", k // 4)], writes=[("sqb2", sl)])
                                    S.op("pe", lambda e, k=k, sl=sl: e.matmul(pb[4][:, 0:N], lhsT=ones, rhs=sqb2[:, sl, 0:N],
                                                                              start=(k == 0), stop=(k == KC - 1)),
                                         reads=[("sqb2", sl), "cbf"], writes=[PB[4]])
                                rstd_from_ss(pb[4][:, 0:N], rstd2[:, 0:N], 1.0 / D, "rstd2", PB[4])
                                for q4 in range(4):
                                    us = ph["urot"].next()
                                    for kk in range(4):
                                        k = q4 * 4 + kk
                                        S.op("dve", lambda e, k=k, kk=kk, us=us: e.scalar_tensor_tensor(
                                            out=ustg[us][:, kk, 0:N], in0=hb[:, k, 0:N], scalar=gcol(l, 2, k), in1=rstd2[:, 0:N],
                                            op0=ALU.mult, op1=ALU.mult), reads=[("hb", q4), "rstd2", "gains"], writes=[("ustg", us)])
                                    S.dma(lambda e, q4=q4, us=us: e.dma_start(
                                        out=u2T_d[q4 * 4:(q4 + 1) * 4, :, c0:c0 + N].rearrange("k p n -> p k n"), in_=ustg[us][:, :, 0:N]),
                                        reads=[("ustg", us)])
                            pending_tail[0] = tail
                        if pending_tail[0] is not None:
                            pending_tail[0]()
                        S.barrier()
                        S.emit()
                        checkpoint("p4")
                    pwo.close()
                    with ExitStack() as p5ab:
                        u2T = sb(p5ab, "u2T", [128, KC, Ts5], BF16)
                        for (gt, gn) in groups:
                            S.dma(lambda e, gt=gt, gn=gn: e.dma_start(
                                out=u2T[:, :, gt * 128 - t0 * 128:(gt + gn) * 128 - t0 * 128],
                                in_=u2T_d[:, :, gt * 128:(gt + gn) * 128].rearrange("k p n -> p k n")),
                                writes=[("u2Tg", gt)])
                        with ExitStack() as p5b:
                            wgu = [sb(p5b, "wgu%d" % i, [128, KC, 1024], BF16) for i in range(2)]
                            sg = [sb(p5b, "sg%d" % i, [128, 512], BF16) for i in range(2)]
                            fst = [sb(p5b, "fst%d" % i, [128, 512], BF16) for i in range(4)]
                            wrot, brot, grot, frot = Rot(2), Rot(2), Rot(2), Rot(4)
                            for fs_ in range(FC // 4):
                                ws = wrot.next()
                                for hf in range(2):
                                    S.dma(lambda e, ws=ws, fs_=fs_, hf=hf: e.dma_start(
                                        out=wgu[ws][:, hf * 8:(hf + 1) * 8, 0:512],
                                        in_=w_gate[l].rearrange("(k p) c -> p k c", p=128)[:, hf * 8:(hf + 1) * 8, fs_ * 512:(fs_ + 1) * 512]),
                                        writes=[("wgu", ws, 0, hf)], q="pool")
                                for hf in range(2):
                                    S.dma(lambda e, ws=ws, fs_=fs_, hf=hf: e.dma_start(
                                        out=wgu[ws][:, hf * 8:(hf + 1) * 8, 512:1024],
                                        in_=w_up[l].rearrange("(k p) c -> p k c", p=128)[:, hf * 8:(hf + 1) * 8, fs_ * 512:(fs_ + 1) * 512]),
                                        writes=[("wgu", ws, 1, hf)], q="pool")
                                for (gt, gn) in groups:
                                    N = gn * 128
                                    u0 = gt * 128 - t0 * 128
                                    for fi in range(4):
                                        f = fs_ * 4 + fi
                                        bp = brot.next() * 2
                                        for c in range(2):
                                            for k in range(KC):
                                                S.op("pe", lambda e, c=c, k=k, bp=bp, ws=ws, N=N, fi=fi, u0=u0: e.matmul(
                                                    pb[bp + c][:, 0:N], lhsT=wgu[ws][:, k, c * 512 + fi * 128:c * 512 + (fi + 1) * 128],
                                                    rhs=u2T[:, k, u0:u0 + N], start=(k == 0), stop=(k == KC - 1)),
                                                    reads=[("wgu", ws, c, k // 8), ("u2Tg", gt)], writes=[PB[bp + c]])
                                        gs = grot.next()
                                        fsl = frot.next()
                                        S.op("act", lambda e, bp=bp, gs=gs, N=N: e.activation(out=sg[gs][:, 0:N], in_=pb[bp][:, 0:N], func=AF.Silu),
                                             writes=[PB[bp], ("sg", gs)])
                                        S.op("dve", lambda e, bp=bp, gs=gs, fsl=fsl, N=N: e.tensor_tensor(
                                            out=fst[fsl][:, 0:N], in0=pb[bp + 1][:, 0:N], in1=sg[gs][:, 0:N], op=ALU.mult),
                                            reads=[("sg", gs)], writes=[PB[bp + 1], ("fst", fsl)])
                                        S.dma(lambda e, fsl=fsl, f=f, gt=gt, N=N: e.dma_start(out=fT_d[f, :, gt * 128:gt * 128 + N], in_=fst[fsl][:, 0:N]),
                                              reads=[("fst", fsl)])
                            S.barrier()
                            S.emit()
                            checkpoint("p5b")
                    with ExitStack() as p5:
                        GM = 6
                        NM = GM * 128
                        fT = sb(p5, "fT", [128, FC, NM], BF16)
                        wd = [sb(p5, "wd%d" % i, [128, FC, 256], BF16) for i in range(2)]
                        hb = sb(p5, "hb", [128, KC, NM], F32)
                        yb = sb(p5, "yb", [128, KC, NM], BF16)
                        sqb = sb(p5, "sqb", [128, 4, NM], BF16)
                        rstd = sb(p5, "rstd", [128, NM], F32)
                        tmpf = sb(p5, "tmpf", [128, NM], F32)
                        ustg = [sb(p5, "ustg%d" % i, [128, 1, NM], BF16) for i in range(2)]
                        urot = Rot(2)
                        drot, brot = Rot(2), Rot(2)
                        for (gt, gn) in token_groups(t0, t1, gmax=GM):
                            N = gn * 128
                            c0 = gt * 128
                            segs = [(0, min(N, 512), 0)]
                            if N > 512:
                                segs.append((512, N - 512, 1))
                            for q4 in range(4):
                                S.dma(lambda e, q4=q4, c0=c0, N=N: e.dma_start(
                                    out=fT[:, q4 * 11:(q4 + 1) * 11, 0:N],
                                    in_=fT_d[q4 * 11:(q4 + 1) * 11, :, c0:c0 + N].rearrange("k p n -> p k n")),
                                    writes=[("fT", q4)])
                            pend = []
                            for ds_ in range(8):
                                dsl = drot.next()
                                for (k0, k1) in ((0, 22), (22, 44)):
                                    S.dma(lambda e, dsl=dsl, ds_=ds_, k0=k0, k1=k1: e.dma_start(
                                        out=wd[dsl][:, k0:k1, :],
                                        in_=w_down[l].rearrange("(k p) c -> p k c", p=128)[:, k0:k1, ds_ * 256:(ds_ + 1) * 256]),
                                        writes=[("wd", dsl, k0)], q="pool")
                                for c in range(2):
                                    dch = ds_ * 2 + c
                                    br = brot.next()
                                    for k in range(FC):
                                        for (sc, sn, sbk) in segs:
                                            bk = br + 2 * sbk
                                            S.op("pe", lambda e, bk=bk, k=k, c=c, dsl=dsl, sc=sc, sn=sn: e.matmul(
                                                pb[bk][:, 0:sn], lhsT=wd[dsl][:, k, c * 128:(c + 1) * 128], rhs=fT[:, k, sc:sc + sn],
                                                start=(k == 0), stop=(k == FC - 1)),
                                                reads=[("wd", dsl, 0), ("wd", dsl, 22), ("fT", k // 11)], writes=[PB[bk]])
                                    sl = dch % 4
                                    for (sc, sn, sbk) in segs:
                                        bk = br + 2 * sbk
                                        S.op("act", lambda e, bk=bk, dch=dch, sc=sc, sn=sn: e.activation(
                                            out=yb[:, dch, sc:sc + sn], in_=pb[bk][:, 0:sn], func=AF.Copy, scale=gcol(l, 3, dch)),
                                            reads=["gains"], writes=[PB[bk], ("yb", dch, sbk)])
                                        S.op("act", lambda e, bk=bk, sl=sl, sc=sc, sn=sn: e.activation(
                                            out=sqb[:, sl, sc:sc + sn], in_=pb[bk][:, 0:sn], func=AF.Square),
                                            writes=[PB[bk], ("sqb", sl, sbk)])

                                        def ssmm(dch=dch, sl=sl, sc=sc, sn=sn, sbk=sbk):
                                            S.op("pe", lambda e: e.matmul(pb[4 + sbk][:, 0:sn], lhsT=ones, rhs=sqb[:, sl, sc:sc + sn],
                                                                          start=(dch == 0), stop=(dch == KC - 1)),
                                                 reads=[("sqb", sl, sbk), "cbf"], writes=[PB[4 + sbk]])
                                        pend.append(ssmm)
                                    while len(pend) > 2 * len(segs):
                                        pend.pop(0)()
                            while pend:
                                pend.pop(0)()
                            for (sc, sn, sbk) in segs:
                                S.op("act", lambda e, sc=sc, sn=sn, sbk=sbk: e.activation(
                                    out=rstd[:, sc:sc + sn], in_=pb[4 + sbk][:, 0:sn], func=AF.Sqrt, scale=1.0 / D, bias=epsc[:, 0:1]),
                                    reads=["epsc"], writes=[PB[4 + sbk], "rstd"])
                            S.op("dve", lambda e, N=N: e.reciprocal(out=rstd[:, 0:N], in_=rstd[:, 0:N]), writes=["rstd"])
                            if c0 == 0:
                                S.op("dve", lambda e: e.tensor_tensor(out=rstd[:, 0:128], in0=rstd[:, 0:128], in1=TMASK[:, 0:128], op=ALU.mult),
                                     reads=["cf"], writes=["rstd"])
                            for q4 in range(4):
                                S.dma(lambda e, q4=q4, c0=c0, N=N: e.dma_start(
                                    out=hb[:, q4 * 4:(q4 + 1) * 4, 0:N],
                                    in_=h_mid[q4 * 4:(q4 + 1) * 4, :, c0:c0 + N].rearrange("k p n -> p k n")),
                                    writes=[("hb", q4)])
                            for k in range(KC):
                                S.op("dve", lambda e, k=k, N=N: e.tensor_tensor(out=tmpf[:, 0:N], in0=yb[:, k, 0:N], in1=rstd[:, 0:N], op=ALU.mult),
                                     reads=["rstd", ("yb", k, 0), ("yb", k, 1)], writes=["tmpf"])
                                S.op("dve", lambda e, k=k, N=N: e.tensor_tensor(out=hb[:, k, 0:N], in0=hb[:, k, 0:N], in1=tmpf[:, 0:N], op=ALU.add),
                                     reads=["tmpf"], writes=[("hb", k // 4)])
                            for q4 in range(4):
                                S.dma(lambda e, q4=q4, c0=c0, N=N: e.dma_start(
                                    out=h_out[q4 * 4:(q4 + 1) * 4, :, c0:c0 + N].rearrange("k p n -> p k n"),
                                    in_=hb[:, q4 * 4:(q4 + 1) * 4, 0:N]),
                                    reads=[("hb", q4)])
                            if l < L - 1:
                                for k in range(KC):
                                    sl = k % 4
                                    for (sc, sn, sbk) in segs:
                                        S.op("act", lambda e, k=k, sl=sl, sc=sc, sn=sn: e.activation(
                                            out=sqb[:, sl, sc:sc + sn], in_=hb[:, k, sc:sc + sn], func=AF.Square),
                                            reads=[("hb", k // 4)], writes=[("sqb", sl, sbk)])
                                        S.op("pe", lambda e, k=k, sl=sl, sc=sc, sn=sn, sbk=sbk: e.matmul(
                                            pb[4 + sbk][:, 0:sn], lhsT=ones, rhs=sqb[:, sl, sc:sc + sn], start=(k == 0), stop=(k == KC - 1)),
                                            reads=[("sqb", sl, sbk), "cbf"], writes=[PB[4 + sbk]])
                                for (sc, sn, sbk) in segs:
                                    S.op("act", lambda e, sc=sc, sn=sn, sbk=sbk: e.activation(
                                        out=rstd[:, sc:sc + sn], in_=pb[4 + sbk][:, 0:sn], func=AF.Sqrt, scale=1.0 / D, bias=epsc[:, 0:1]),
                                        reads=["epsc"], writes=[PB[4 + sbk], "rstd"])
                                S.op("dve", lambda e, N=N: e.reciprocal(out=rstd[:, 0:N], in_=rstd[:, 0:N]), writes=["rstd"])
                                for k in range(KC):
                                    us = urot.next()
                                    S.op("dve", lambda e, k=k, us=us, N=N: e.scalar_tensor_tensor(
                                        out=ustg[us][:, 0, 0:N], in0=hb[:, k, 0:N], scalar=gcol(l + 1, 0, k), in1=rstd[:, 0:N],
                                        op0=ALU.mult, op1=ALU.mult), reads=[("hb", k // 4), "rstd", "gains"], writes=[("ustg", us)])
                                    S.dma(lambda e, k=k, us=us, c0=c0, N=N: e.dma_start(
                                        out=uT_d[k, :, c0:c0 + N], in_=ustg[us][:, 0, 0:N]),
                                        reads=[("ustg", us)])
                        S.barrier()
                        S.emit()
                        checkpoint("p5")
        except _Stop:
            pass
        S.barrier()
        S.emit()
    return nc


def make_consts(T, half, tok_off):
    pos = (np.arange(T) + tok_off - PADF).astype(np.float32)
    inv = (10000.0 ** (-np.arange(128, dtype=np.float32) / 128)).astype(np.float32)
    ang = pos[None, :] * inv[:, None]
    cr, sr = np.cos(ang).astype(np.float32), np.sin(ang).astype(np.float32)
    ropeR = np.stack([cr, sr, cr / 16.0, sr / 16.0]).astype(np.float32)
    inva = (500000.0 ** (-np.arange(16, dtype=np.float32) / 16)).astype(np.float32)
    anga = pos[None, :] * inva[:, None]
    ca, sa = np.cos(anga).astype(np.float32), np.sin(anga).astype(np.float32)
    ropeA = np.stack([np.concatenate([ca, ca], 0), np.concatenate([-sa, sa], 0)]).astype(np.float32)
    cbf = np.zeros((128, 128 * 2 + 32 + 512 * 5), np.float32)
    cbf[:, 0:128] = 1.0
    cbf[:, 128:256] = np.eye(128)
    for m in range(32):
        src = m + 16 if m < 16 else m - 16
        cbf[src, 256 + m] = 1.0
    lk = np.arange(128)[:, None]
    lq = np.arange(128)[None, :]
    ge = np.tile((lk >= lq), (1, 4)).astype(np.float32)
    le = np.tile((lk <= lq), (1, 4)).astype(np.float32)
    cbf[:, 288:800] = np.tile((lk >= PADF) * np.ones((1, 128)), (1, 4))
    cbf[:, 800:1312] = ge
    cbf[:, 1312:1824] = le
    if half == 1:
        cbf[:, 1824:2336] = ge
    else:
        cbf[:, 2336:2848] = le
    cf = np.zeros((128, 128 * 6 + 8 + 128), np.float32)
    j = np.arange(128)[:, None].astype(np.float32)
    i = np.arange(128)[None, :].astype(np.float32)
    cf[:, 0:128] = np.maximum(i - j, 0)
    cf[:, 128:256] = (i >= j)
    cf[:, 256:384] = np.maximum(j - i, 0)
    cf[:, 384:512] = (j > i)
    cf[:, 512:640] = i + 1.0
    cf[:, 640:768] = 128.0 - i
    cf[:, 768] = 127.0 - np.arange(128)
    cf[:, 769] = np.arange(128)
    cf[:, 770] = 128.0
    cf[:, 771] = 1.0 - half
    cf[:, 772] = float(half)
    if half == 0:
        cf[:, 776:904] = (i >= PADF) * np.ones((128, 1))
    return ropeR, ropeA, cbf, cf


def kernel(x, meta_tokens, w_in, w_out, attn_sink, ret_decay_fwd, ret_decay_bwd, ret_norm,
           norm_mix_pre, norm_mix_post, w_gate, w_up, w_down, norm_ffn_pre, norm_ffn_post, _L=None):
    x = np.asarray(x, np.float32)
    B, SEQ, _ = x.shape
    L = int(_L) if _L is not None else w_in.shape[0]
    HS = SEQ // 2
    NT = HS // 128 + 1
    T = NT * 128
    f32 = lambda a: np.ascontiguousarray(np.asarray(a, np.float32))
    g4 = np.stack([f32(norm_mix_pre)[:L], f32(norm_mix_post)[:L], f32(norm_ffn_pre)[:L], f32(norm_ffn_post)[:L]], 1)
    gains = np.ascontiguousarray(g4.reshape(L, 4, KC, 128).transpose(3, 0, 1, 2).reshape(128, L * 4 * KC))
    rnl = np.ascontiguousarray(f32(ret_norm)[:L].reshape(L, 8, 128).transpose(2, 0, 1).reshape(128, L * 8))
    sd = np.concatenate([f32(attn_sink)[:L], f32(ret_decay_fwd)[:L], f32(ret_decay_bwd)[:L]], 1).reshape(1, L * 16)
    sd = np.ascontiguousarray(np.broadcast_to(sd, (128, L * 16)))
    shared = {
        "w_in": f32(w_in)[:L], "w_out": f32(w_out)[:L], "w_gate": f32(w_gate)[:L], "w_up": f32(w_up)[:L], "w_down": f32(w_down)[:L],
        "gains": gains, "retnorm": rnl, "sinkdecay": sd,
    }
    mt = f32(meta_tokens)
    consts = [make_consts(T, 0, 0), make_consts(T, 1, HS)]
    in_maps = []
    for b in range(B):
        for half in range(2):
            m = dict(shared)
            h0 = np.zeros((T, D), np.float32)
            if half == 0:
                h0[PADF:PADF + N_META] = mt
            h0[128:] = x[b, half * HS:(half + 1) * HS]
            m["h0T"] = np.ascontiguousarray(h0.T.reshape(KC, 128, T))
            m["ropeR"], m["ropeA"], m["cbf"], m["cf"] = consts[half]
            in_maps.append(m)
    nc = build_program(L=L, NT=NT)
    res = run_bass_kernel_spmd(nc, in_maps, core_ids=list(range(2 * B)))
    out = np.empty((B, SEQ, D), np.float32)
    for b in range(B):
        for half in range(2):
            oT = np.asarray(res.results[2 * b + half]["outT"]).reshape(D, T)
            out[b, half * HS:(half + 1) * HS] = oT[:, 128:].T
    return out
```

```python
import os
from contextlib import ExitStack
import numpy as np
import ml_dtypes
import concourse.bass as bass
import concourse.mybir as mybir
from concourse.bass_utils import run_bass_kernel_spmd

F32 = mybir.dt.float32
BF16 = mybir.dt.bfloat16
ALU = mybir.AluOpType
AF = mybir.ActivationFunctionType

D = 2048
KC = 16
DFF = 5632
FC = 44
INC = 5632
N_META = 16
PADF = 112
EPS = 1e-6
ENGS = ("pe", "act", "dve", "pool", "sp")


class Sched:
    def __init__(self, nc, stack, n_dma_sems=32, same_engine_sync=True):
        self.nc = nc
        self.same_engine_sync = same_engine_sync
        self.esem = {e: stack.enter_context(nc.semaphore("s_" + e)) for e in ENGS}
        self.cnt = {e: 0 for e in ENGS}
        self.prog = {e: [] for e in ENGS}
        self.seen = {e: {} for e in ENGS}
        self.dsem = [stack.enter_context(nc.semaphore("d_%d" % i)) for i in range(n_dma_sems)]
        self.dcnt = [0] * n_dma_sems
        self.nsp = (n_dma_sems * 5) // 8
        self.drr = {"sp": 0, "pool": 0}
        self.res_w = {}
        self.res_r = {}
        self.semkey = {}
        for e in ENGS:
            self.semkey[id(self.esem[e])] = ("e", e)
        for i, s in enumerate(self.dsem):
            self.semkey[id(s)] = ("d", i)
        self.nops = 0
        self.dead = False
        self.csem = stack.enter_context(nc.semaphore("c_0"))
        self.semkey[id(self.csem)] = ("c", 0)
        self.ccnt = 0

    def _key(self, sem):
        return self.semkey[id(sem)]

    def _deps(self, reads, writes):
        deps = {}

        def add(ev):
            k = self._key(ev[0])
            if k not in deps or deps[k][1] < ev[1]:
                deps[k] = ev

        for r in reads:
            if r in self.res_w:
                add(self.res_w[r])
        for w in writes:
            if w in self.res_w:
                add(self.res_w[w])
            for ev in self.res_r.get(w, {}).values():
                add(ev)
        return deps

    def _emit_waits(self, eng, deps):
        for k, ev in deps.items():
            if k == ("e", eng):
                if eng in ("pe", "sp") or not self.same_engine_sync:
                    continue
            if self.seen[eng].get(k, 0) >= ev[1]:
                continue
            self.seen[eng][k] = ev[1]
            self.prog[eng].append(("wait", ev[0], ev[1]))

    def _record(self, ev, reads, writes):
        k = self._key(ev[0])
        for r in reads:
            self.res_r.setdefault(r, {})[k] = ev
        for w in writes:
            self.res_w[w] = ev
            self.res_r[w] = {}

    def op(self, eng, fn, reads=(), writes=()):
        if self.dead:
            return None
        deps = self._deps(reads, writes)
        self._emit_waits(eng, deps)
        self.cnt[eng] += 1
        ev = (self.esem[eng], self.cnt[eng])
        self.prog[eng].append(("op", fn, ev[0], 1))
        self._record(ev, reads, writes)
        self.nops += 1
        return ev

    def dma(self, fn, reads=(), writes=(), q="sp"):
        if self.dead:
            return None
        deps = self._deps(reads, writes)
        if q == "sp":
            i = self.drr["sp"]
            self.drr["sp"] = (i + 1) % self.nsp
        else:
            i = self.nsp + self.drr["pool"]
            self.drr["pool"] = (self.drr["pool"] + 1) % (len(self.dsem) - self.nsp)
        if self.dcnt[i] > 0:
            k = ("d", i)
            prev = (self.dsem[i], 16 * self.dcnt[i])
            if k not in deps or deps[k][1] < prev[1]:
                deps[k] = prev
        self._emit_waits(q, deps)
        self.dcnt[i] += 1
        ev = (self.dsem[i], 16 * self.dcnt[i])
        self.prog[q].append(("op", fn, ev[0], 16))
        self._record(ev, reads, writes)
        self.nops += 1
        return ev

    def coll(self, fn, reads=(), writes=()):
        if self.dead:
            return None
        deps = self._deps(reads, writes)
        self._emit_waits("pool", deps)
        self.ccnt += 1
        ev = (self.csem, self.ccnt)
        self.prog["pool"].append(("op", fn, self.csem, 1))
        self._record(ev, reads, writes)
        return ev

    def barrier(self, skip_coll=False):
        evs = {}
        if self.ccnt > 0 and not skip_coll:
            evs[("c", 0)] = (self.csem, self.ccnt)
        for e in ENGS:
            if self.cnt[e] > 0:
                evs[("e", e)] = (self.esem[e], self.cnt[e])
        for i, s in enumerate(self.dsem):
            if self.dcnt[i] > 0:
                evs[("d", i)] = (s, 16 * self.dcnt[i])
        for e in ENGS:
            d = {k: v for k, v in evs.items() if k != ("e", e)}
            self._emit_waits(e, d)

    def emit(self):
        nc = self.nc
        prog = self.prog

        def replay(engobj, items):
            for it in items:
                if it[0] == "wait":
                    engobj.wait_ge(it[1], it[2])
                else:
                    it[1](engobj).then_inc(it[2], it[3])

        with nc.Block() as block:
            @block.tensor
            def _(e):
                replay(e, prog["pe"])

            @block.scalar
            def _(e):
                replay(e, prog["act"])

            @block.vector
            def _(e):
                replay(e, prog["dve"])

            @block.gpsimd
            def _(e):
                replay(e, prog["pool"])

            @block.sync
            def _(e):
                replay(e, prog["sp"])
        self.prog = {e: [] for e in ENGS}


class Rot:
    def __init__(self, n):
        self.n = n
        self.i = 0

    def next(self):
        v = self.i
        self.i = (self.i + 1) % self.n
        return v


def token_groups(t0, t1, gmax=4):
    out = []
    t = t0
    while t < t1:
        n = min(gmax, t1 - t)
        out.append((t, n))
        t += n
    return out


class _Stop(Exception):
    pass


def build_program(L=4, NT=33, supers=None, debug=False):
    stop_after = os.environ.get("KSTOP", "")
    T = NT * 128
    if supers is None:
        supers = [(0, NT)]
    nc = bass.Bass("TRN2", target_bir_lowering=False)
    okind = "ExternalOutput" if debug else "Internal"

    def dram(name, shape, dt, kind="Internal"):
        return nc.dram_tensor(name, shape, dt, kind=kind).ap()

    h0T = dram("h0T", [KC, 128, T], F32, "ExternalInput")
    w_in = dram("w_in", [L, D, INC], F32, "ExternalInput")
    w_out = dram("w_out", [L, D, D], F32, "ExternalInput")
    w_gate = dram("w_gate", [L, D, DFF], F32, "ExternalInput")
    w_up = dram("w_up", [L, D, DFF], F32, "ExternalInput")
    w_down = dram("w_down", [L, DFF, D], F32, "ExternalInput")
    gains_d = dram("gains", [128, L * 4 * KC], F32, "ExternalInput")
    rn_d = dram("retnorm", [128, L * 8], F32, "ExternalInput")
    sd_d = dram("sinkdecay", [128, L * 16], F32, "ExternalInput")
    ropeR_d = dram("ropeR", [4, 128, T], F32, "ExternalInput")
    ropeA_d = dram("ropeA", [2, 32, T], F32, "ExternalInput")
    cbf_d = dram("cbf", [128, 128 * 2 + 32 + 512 * 5], F32, "ExternalInput")
    cf_d = dram("cf", [128, 128 * 6 + 8 + 128], F32, "ExternalInput")
    outT = dram("outT", [KC, 128, T], F32, "ExternalOutput")
    hX = dram("hX", [KC, 128, T], F32)
    hY = dram("hY", [KC, 128, T], F32)
    aqT = dram("aqT", [8, 128, T], BF16, okind)
    akT = dram("akT", [2, 128, T], BF16, okind)
    av2 = dram("av2", [2, NT, 128, 128], BF16, okind)
    rqT = dram("rqT", [8, 128, T], BF16, okind)
    rkT = dram("rkT", [8, 128, T], BF16, okind)
    rkM = dram("rkM", [NT, 128, 1024], BF16, okind)
    rvM = dram("rvM", [NT, 128, 1024], BF16, okind)
    rgM = dram("rgM", [NT, 128, 1024], BF16, okind)
    mixT = dram("mixT", [KC, 128, T], BF16, okind)
    fT_d = dram("fT_d", [FC, 128, T], BF16)
    uT_d = dram("uT_d", [KC, 128, T], BF16)
    u2T_d = dram("u2T_d", [KC, 128, T], BF16)
    XA = 12 * 128
    xs_att = dram("xs_att", [XA, 128], BF16)
    xr_att = dram("xr_att", [2 * XA, 128], BF16)
    XR = 8 * 128
    xs_ret = dram("xs_ret", [XR, 512], F32)
    xr_ret = dram("xr_ret", [2 * XR, 512], F32)
    RG = [[0, 1], [2, 3], [4, 5], [6, 7]]
    HALO = {0: 0, 1: 1, NT - 1: 2}

    def xblk(g, ti, kind):
        return ((g * 3 + ti) * 2 + kind) * 128

    with ExitStack() as st:
        S = Sched(nc, st)

        uid = [0]

        def sb(stack, name, shape, dt):
            uid[0] += 1
            return stack.enter_context(nc.sbuf_tensor("sb%d_%s" % (uid[0], name), shape, dt))

        def psum(stack, name, shape, dt):
            uid[0] += 1
            return stack.enter_context(nc.psum_tensor("ps%d_%s" % (uid[0], name), shape, dt))

        gains = sb(st, "gains", [128, L * 4 * KC], F32)
        rn = sb(st, "rn", [128, L * 8], F32)
        sdraw = sb(st, "sdraw", [128, L * 16], F32)
        sdexp = sb(st, "sdexp", [128, L * 16], F32)
        cbf = sb(st, "cbf", [128, 128 * 2 + 32 + 512 * 5], BF16)
        cf = sb(st, "cf", [128, 128 * 6 + 8 + 128], F32)
        epsc = sb(st, "epsc", [128, 1], F32)
        ones = cbf[:, 0:128]
        ident = cbf[:, 128:256]
        perm = cbf[:, 256:288]
        masks = {"meta": cbf[:, 288:800], "ge": cbf[:, 800:1312], "le": cbf[:, 1312:1824],
                 "xprev": cbf[:, 1824:2336], "xnext": cbf[:, 2336:2848]}
        P1 = cf[:, 0:128]
        MF = cf[:, 128:256]
        P2 = cf[:, 256:384]
        MB = cf[:, 384:512]
        ROW1 = cf[:, 512:640]
        ROW2 = cf[:, 640:768]
        COLS = cf[:, 768:776]
        TMASK = cf[:, 776:904]
        pb = [psum(st, "pb%d" % i, [128, 512], F32) for i in range(6)]
        pbf = [psum(st, "pbf%d" % i, [128, 1024], BF16) for i in range(2)]
        PB = [("pb", i) for i in range(6)]
        PBF = [("pbf", i) for i in range(2)]

        initst = ExitStack()
        cbf32 = sb(initst, "cbf32", [128, 128 * 2 + 32 + 512 * 5], F32)
        S.dma(lambda e: e.dma_start(out=gains[:], in_=gains_d), writes=["gains"])
        S.dma(lambda e: e.dma_start(out=rn[:], in_=rn_d), writes=["rn"])
        S.dma(lambda e: e.dma_start(out=sdraw[:], in_=sd_d), writes=["sdraw"])
        S.dma(lambda e: e.dma_start(out=cbf32[:], in_=cbf_d), writes=["cbf32"])
        S.dma(lambda e: e.dma_start(out=cf[:], in_=cf_d), writes=["cf"])
        S.op("pool", lambda e: e.tensor_copy(out=cbf[:], in_=cbf32[:]), reads=["cbf32"], writes=["cbf"])
        S.op("pool", lambda e: e.memset(epsc[:], EPS), writes=["epsc"])
        S.op("act", lambda e: e.activation(out=sdexp[:], in_=sdraw[:], func=AF.Exp), reads=["sdraw"], writes=["sdexp"])
        for l in range(L):
            S.op("dve", lambda e, l=l: e.tensor_scalar(out=sdexp[:, l * 16 + 8:l * 16 + 16], in0=sdexp[:, l * 16 + 8:l * 16 + 16],
                                                      scalar1=-1.0, scalar2=None, op0=ALU.mult),
                 reads=["sdexp"], writes=["sdexp"])
        S.barrier()
        S.emit()
        initst.close()

        def gcol(l, j, k):
            c = (l * 4 + j) * KC + k
            return gains[:, c:c + 1]

        def rstd_from_ss(ss_bank_ap, rstd_ap, inv_n, tag, bankres, extra_mask=None):
            S.op("act", lambda e: e.activation(out=rstd_ap, in_=ss_bank_ap, func=AF.Sqrt, scale=inv_n, bias=epsc[:, 0:1]),
                 reads=["epsc"], writes=[bankres, tag])
            S.op("dve", lambda e: e.reciprocal(out=rstd_ap, in_=rstd_ap), writes=[tag])
            if extra_mask is not None:
                S.op("dve", lambda e: e.tensor_tensor(out=rstd_ap[:, 0:128], in0=rstd_ap[:, 0:128], in1=extra_mask, op=ALU.mult), writes=[tag])

        def prenorm(ph, hsrc, c0, N, l, j, ut, ucol0, tagp):
            hb = ph["hb"]
            sqb = ph["sqb"]
            rstd = ph["rstd"]
            for q4 in range(4):
                S.dma(lambda e, q4=q4: e.dma_start(out=hb[:, q4 * 4:(q4 + 1) * 4, 0:N],
                                                    in_=hsrc[q4 * 4:(q4 + 1) * 4, :, c0:c0 + N].rearrange("k p n -> p k n")),
                      writes=[("hb", q4)])
            for k in range(KC):
                sl = k % 4
                S.op("act", lambda e, k=k, sl=sl: e.activation(out=sqb[:, sl, 0:N], in_=hb[:, k, 0:N], func=AF.Square),
                     reads=[("hb", k // 4)], writes=[("sqb", sl)])
                S.op("pe", lambda e, k=k, sl=sl: e.matmul(pb[5][:, 0:N], lhsT=ones, rhs=sqb[:, sl, 0:N],
                                                          start=(k == 0), stop=(k == KC - 1)),
                     reads=[("sqb", sl), "cbf"], writes=[PB[5]])
            rstd_from_ss(pb[5][:, 0:N], rstd[:, 0:N], 1.0 / D, "rstd", PB[5])
            for k in range(KC):
                eng = "dve"
                if eng == "dve":
                    S.op("dve", lambda e, k=k: e.scalar_tensor_tensor(out=ut[:, k, ucol0:ucol0 + N], in0=hb[:, k, 0:N],
                                                                     scalar=gcol(l, j, k), in1=rstd[:, 0:N],
                                                                     op0=ALU.mult, op1=ALU.mult),
                         reads=[("hb", k // 4), "rstd", "gains"], writes=[(tagp, k)])
                else:
                    ptmp = ph["ptmp"]
                    S.op("pool", lambda e, k=k: e.tensor_tensor(out=ptmp[:, 0:N], in0=hb[:, k, 0:N], in1=rstd[:, 0:N], op=ALU.mult),
                         reads=["rstd", ("hb", k // 4)], writes=["ptmp"])
                    S.op("pool", lambda e, k=k: e.tensor_scalar(out=ut[:, k, ucol0:ucol0 + N], in0=ptmp[:, 0:N],
                                                                scalar1=gcol(l, j, k), scalar2=None, op0=ALU.mult),
                         reads=["ptmp", "gains"], writes=[(tagp, k)])
            return hb

        def postnorm_residual(ph, hsrc, hdst, c0, N, l, j, reload_h):
            hb = ph["hb"]
            yb = ph["yb"]
            rstd = ph["rstd"]
            flush_ss()
            rstd_from_ss(pb[5][:, 0:N], rstd[:, 0:N], 1.0 / D, "rstd", PB[5],
                         extra_mask=(TMASK[:, 0:128] if c0 == 0 else None))
            if c0 == 0 and N > 128:
                pass
            if reload_h:
                for q4 in range(4):
                    S.dma(lambda e, q4=q4: e.dma_start(out=hb[:, q4 * 4:(q4 + 1) * 4, 0:N],
                                                        in_=hsrc[q4 * 4:(q4 + 1) * 4, :, c0:c0 + N].rearrange("k p n -> p k n")),
                          writes=[("hb", q4)])
            for k in range(KC):
                if "tmpf" in ph:
                    tmpf = ph["tmpf"]
                    S.op("dve", lambda e, k=k: e.tensor_tensor(out=tmpf[:, 0:N], in0=yb[:, k, 0:N], in1=rstd[:, 0:N], op=ALU.mult),
                         reads=["rstd", (ph.get("ybn", "yb"), k)], writes=["tmpf"])
                    S.op("dve", lambda e, k=k: e.tensor_tensor(out=hb[:, k, 0:N], in0=hb[:, k, 0:N], in1=tmpf[:, 0:N], op=ALU.add),
                         reads=["tmpf"], writes=[("hb", k // 4)])
                else:
                    S.op("dve", lambda e, k=k: e.tensor_tensor(out=yb[:, k, 0:N], in0=yb[:, k, 0:N], in1=rstd[:, 0:N], op=ALU.mult),
                         reads=["rstd"], writes=[("yb", k)])
                    S.op("dve", lambda e, k=k: e.tensor_tensor(out=hb[:, k, 0:N], in0=hb[:, k, 0:N], in1=yb[:, k, 0:N], op=ALU.add),
                         reads=[("yb", k)], writes=[("hb", k // 4)])
            for q4 in range(4):
                S.dma(lambda e, q4=q4: e.dma_start(out=hdst[q4 * 4:(q4 + 1) * 4, :, c0:c0 + N].rearrange("k p n -> p k n"),
                                                    in_=hb[:, q4 * 4:(q4 + 1) * 4, 0:N]),
                      reads=[("hb", q4)])

        def prenorm_hb(ph, N, l, j, dst, c0):
            hb, sqb, rstd, ustg = ph["hb"], ph["sqb"], ph["rstd"], ph["ustg"]
            for k in range(KC):
                sl = k % 4
                S.op("act", lambda e, k=k, sl=sl: e.activation(out=sqb[:, sl, 0:N], in_=hb[:, k, 0:N], func=AF.Square),
                     reads=[("hb", k // 4)], writes=[("sqb", sl)])
                S.op("pe", lambda e, k=k, sl=sl: e.matmul(pb[5][:, 0:N], lhsT=ones, rhs=sqb[:, sl, 0:N],
                                                          start=(k == 0), stop=(k == KC - 1)),
                     reads=[("sqb", sl), "cbf"], writes=[PB[5]])
            rstd_from_ss(pb[5][:, 0:N], rstd[:, 0:N], 1.0 / D, "rstd", PB[5])
            for q4 in range(4):
                us = ph["urot"].next()
                for kk in range(4):
                    k = q4 * 4 + kk
                    S.op("dve", lambda e, k=k, kk=kk, us=us: e.scalar_tensor_tensor(
                        out=ustg[us][:, kk, 0:N], in0=hb[:, k, 0:N], scalar=gcol(l, j, k), in1=rstd[:, 0:N],
                        op0=ALU.mult, op1=ALU.mult), reads=[("hb", q4), "rstd", "gains"], writes=[("ustg", us)])
                S.dma(lambda e, q4=q4, us=us: e.dma_start(
                    out=dst[q4 * 4:(q4 + 1) * 4, :, c0:c0 + N].rearrange("k p n -> p k n"), in_=ustg[us][:, :, 0:N]),
                    reads=[("ustg", us)])

        pend_ss = []

        def flush_ss(keep=0):
            while len(pend_ss) > keep:
                pend_ss.pop(0)()

        def evac_y(ph, bank, dchunk, N, l, j):
            yb = ph["yb"]
            sqb = ph["sqb"]
            sl = dchunk % 4
            S.op("act", lambda e: e.activation(out=yb[:, dchunk, 0:N], in_=pb[bank][:, 0:N], func=AF.Copy, scale=gcol(l, j, dchunk)),
                 reads=["gains"], writes=[PB[bank], (ph.get("ybn", "yb"), dchunk)])
            S.op("act", lambda e: e.activation(out=sqb[:, sl, 0:N], in_=pb[bank][:, 0:N], func=AF.Square),
                 writes=[PB[bank], ("sqb", sl)])
            pend_ss.append(lambda: S.op("pe", lambda e: e.matmul(pb[5][:, 0:N], lhsT=ones, rhs=sqb[:, sl, 0:N],
                                                               start=(dchunk == 0), stop=(dchunk == KC - 1)),
                                        reads=[("sqb", sl), "cbf"], writes=[PB[5]]))
            flush_ss(keep=2)

        def checkpoint(tag):
            if stop_after and stop_after == tag:
                S.dead = True

        try:
            for l in range(L if not stop_after else 1):
                h_in = h0T if l == 0 else hY
                h_mid = hX
                h_out = outT if l == L - 1 else hY

                for (t0, t1) in supers:
                    Ts = (t1 - t0) * 128
                    s0 = t0 * 128
                    groups = token_groups(t0, t1)
                    with ExitStack() as p12:
                        uT = sb(p12, "uT", [128, KC, Ts], BF16)
                        if l == 0:
                            with ExitStack() as p1:
                                ph = {"hb": sb(p1, "hb", [128, KC, 512], F32), "sqb": sb(p1, "sqb", [128, 4, 512], BF16),
                                      "rstd": sb(p1, "rstd", [128, 512], F32), "ptmp": sb(p1, "ptmp", [128, 512], F32)}
                                for (gt, gn) in groups:
                                    prenorm(ph, h_in, gt * 128, gn * 128, l, 0, uT, gt * 128 - s0, "uT")
                                S.barrier()
                                S.emit()
                                checkpoint("p1")
                        else:
                            for (gt, gn) in groups:
                                S.dma(lambda e, gt=gt, gn=gn: e.dma_start(
                                    out=uT[:, :, gt * 128 - s0:(gt + gn) * 128 - s0],
                                    in_=uT_d[:, :, gt * 128:(gt + gn) * 128].rearrange("k p n -> p k n")),
                                    writes=[("uTg", gt)])
                        with ExitStack() as p2:
                            wsl = [sb(p2, "wsl%d" % i, [128, KC, 512], BF16) for i in range(3)]
                            ropeR = sb(p2, "ropeR", [128, 4, Ts], F32)
                            ropeA = sb(p2, "ropeA", [32, 2, Ts], F32)
                            stg = [sb(p2, "stg%d" % i, [128, 2, 512], BF16) for i in range(3)]
                            stgT = [sb(p2, "stgT%d" % i, [128, 4, 512], BF16) for i in range(3)]
                            tmp = [sb(p2, "tmp%d" % i, [128, 512], F32) for i in range(4)]
                            wrot, srot, trot = Rot(3), Rot(3), Rot(3)
                            S.dma(lambda e: e.dma_start(out=ropeR[:], in_=ropeR_d[:, :, s0:s0 + Ts].rearrange("a p n -> p a n")), writes=["ropeR"])
                            S.dma(lambda e: e.dma_start(out=ropeA[:], in_=ropeA_d[:, :, s0:s0 + Ts].rearrange("a p n -> p a n")), writes=["ropeA"])
                            bankrot = Rot(2)
                            for si in [4, 5, 0, 1, 2, 3] + list(range(10, 14)) + [100, 101] + list(range(6, 10)) + [102, 103]:
                                ws = wrot.next()
                                if si < 100:
                                    wc0, wcn = si * 256, 256
                                else:
                                    wc0, wcn = (3584 if si < 102 else 4608) + (si % 2) * 512, 512
                                S.dma(lambda e, ws=ws, wc0=wc0, wcn=wcn: e.dma_start(
                                    out=wsl[ws][:, :, 0:wcn], in_=w_in[l].rearrange("(k p) c -> p k c", p=128)[:, :, wc0:wc0 + wcn]),
                                    writes=[("wsl", ws)], q="pool")
                                if si <= 4 or 6 <= si <= 13:
                                    for (gt, gn) in groups:
                                        N = gn * 128
                                        u0 = gt * 128 - s0
                                        bp = bankrot.next() * 2
                                        for c in range(2):
                                            for k in range(KC):
                                                S.op("pe", lambda e, c=c, k=k, bp=bp, ws=ws, u0=u0, N=N: e.matmul(
                                                    pb[bp + c][:, 0:N], lhsT=wsl[ws][:, k, c * 128:(c + 1) * 128],
                                                    rhs=uT[:, k, u0:u0 + N], start=(k == 0), stop=(k == KC - 1)),
                                                    reads=[("wsl", ws), ("uT", k), ("uTg", gt)], writes=[PB[bp + c]])
                                        sg = srot.next()
                                        if si <= 4:
                                            for c in range(2):
                                                S.op("act", lambda e, c=c, bp=bp, sg=sg, N=N: e.activation(
                                                    out=stg[sg][:, c, 0:N], in_=pb[bp + c][:, 0:N], func=AF.Copy),
                                                    writes=[PB[bp + c], ("stg", sg, c)])
                                                S.op("pe", lambda e, c=c, sg=sg, N=N: e.matmul(
                                                    pb[4][0:32, 0:N], lhsT=perm, rhs=stg[sg][:, c, 0:N], start=True, stop=True),
                                                    reads=[("stg", sg, c), "cbf"], writes=[PB[4]])
                                                S.op("dve", lambda e, c=c, bp=bp, N=N, u0=u0: e.tensor_tensor(
                                                    out=tmp[0][0:32, 0:N], in0=pb[bp + c][0:32, 0:N], in1=ropeA[:, 0, u0:u0 + N], op=ALU.mult),
                                                    reads=["ropeA"], writes=[PB[bp + c], ("tmp", 0)])
                                                S.op("dve", lambda e, N=N, u0=u0: e.tensor_tensor(
                                                    out=tmp[1][0:32, 0:N], in0=pb[4][0:32, 0:N], in1=ropeA[:, 1, u0:u0 + N], op=ALU.mult),
                                                    reads=["ropeA"], writes=[PB[4], ("tmp", 1)])
                                                S.op("dve", lambda e, c=c, sg=sg, N=N: e.tensor_tensor(
                                                    out=stg[sg][0:32, c, 0:N], in0=tmp[0][0:32, 0:N], in1=tmp[1][0:32, 0:N], op=ALU.add),
                                                    reads=[("tmp", 0), ("tmp", 1)], writes=[("stg", sg, c)])
                                            dst = aqT[2 * si:2 * si + 2] if si < 4 else akT
                                            S.dma(lambda e, sg=sg, N=N, dst=dst, gt=gt: e.dma_start(
                                                out=dst[:, :, gt * 128:gt * 128 + N].rearrange("h p n -> p h n"), in_=stg[sg][:, :, 0:N]),
                                                reads=[("stg", sg, 0), ("stg", sg, 1)])
                                            if si == 4:
                                                for jj in range(gn):
                                                    if (gt + jj) in HALO:
                                                        for g in range(2):
                                                            r0 = xblk(g, HALO[gt + jj], 0)
                                                            S.dma(lambda e, sg=sg, g=g, jj=jj, r0=r0: e.dma_start(
                                                                out=xs_att[r0:r0 + 128, :], in_=stg[sg][:, g, jj * 128:(jj + 1) * 128]),
                                                                reads=[("stg", sg, g)], writes=[("xs_att", r0)])
                                        else:
                                            isk = si >= 10
                                            ci, sn = (2, 3) if isk else (0, 1)
                                            hh = (si - 10) if isk else (si - 6)
                                            b0, b1 = bp, bp + 1
                                            S.op("dve", lambda e, N=N, u0=u0, b0=b0, ci=ci: e.tensor_tensor(
                                                out=tmp[0][:, 0:N], in0=pb[b0][:, 0:N], in1=ropeR[:, ci, u0:u0 + N], op=ALU.mult),
                                                reads=["ropeR"], writes=[PB[b0], ("tmp", 0)])
                                            S.op("dve", lambda e, N=N, u0=u0, b1=b1, sn=sn: e.tensor_tensor(
                                                out=tmp[1][:, 0:N], in0=pb[b1][:, 0:N], in1=ropeR[:, sn, u0:u0 + N], op=ALU.mult),
                                                reads=["ropeR"], writes=[PB[b1], ("tmp", 1)])
                                            S.op("dve", lambda e, N=N, sg=sg: e.tensor_tensor(
                                                out=stg[sg][:, 0, 0:N], in0=tmp[0][:, 0:N], in1=tmp[1][:, 0:N], op=ALU.subtract),
                                                reads=[("tmp", 0), ("tmp", 1)], writes=[("stg", sg, 0)])
                                            S.op("dve", lambda e, N=N, u0=u0, b1=b1, ci=ci: e.tensor_tensor(
                                                out=tmp[2][:, 0:N], in0=pb[b1][:, 0:N], in1=ropeR[:, ci, u0:u0 + N], op=ALU.mult),
                                                reads=["ropeR"], writes=[PB[b1], ("tmp", 2)])
                                            S.op("dve", lambda e, N=N, u0=u0, b0=b0, sn=sn: e.tensor_tensor(
                                                out=tmp[3][:, 0:N], in0=pb[b0][:, 0:N], in1=ropeR[:, sn, u0:u0 + N], op=ALU.mult),
                                                reads=["ropeR"], writes=[PB[b0], ("tmp", 3)])
                                            S.op("dve", lambda e, N=N, sg=sg: e.tensor_tensor(
                                                out=stg[sg][:, 1, 0:N], in0=tmp[2][:, 0:N], in1=tmp[3][:, 0:N], op=ALU.add),
                                                reads=[("tmp", 2), ("tmp", 3)], writes=[("stg", sg, 1)])
                                            dst = rkT if isk else rqT
                                            S.dma(lambda e, sg=sg, N=N, dst=dst, gt=gt, hh=hh: e.dma_start(
                                                out=dst[2 * hh:2 * hh + 2, :, gt * 128:gt * 128 + N].rearrange("h p n -> p h n"),
                                                in_=stg[sg][:, :, 0:N]),
                                                reads=[("stg", sg, 0), ("stg", sg, 1)])
                                            if isk:
                                                tg = trot.next()
                                                fb = tg % 2
                                                for jj in range(gn):
                                                    for c in range(2):
                                                        S.op("pe", lambda e, jj=jj, c=c, sg=sg, fb=fb: e.transpose(
                                                            out=pbf[fb][:, jj * 256 + c * 128:jj * 256 + (c + 1) * 128],
                                                            in_=stg[sg][:, c, jj * 128:(jj + 1) * 128], identity=ident),
                                                            reads=[("stg", sg, c), "cbf"], writes=[PBF[fb]])
                                                S.op("act", lambda e, tg=tg, fb=fb, gn=gn: e.activation(
                                                    out=stgT[tg][:, 0:gn, 0:256], in_=pbf[fb][:, 0:gn * 256].rearrange("p (t c) -> p t c", c=256),
                                                    func=AF.Copy), writes=[PBF[fb], ("stgT", tg)])
                                                S.dma(lambda e, tg=tg, gn=gn, gt=gt, hh=hh: e.dma_start(
                                                    out=rkM[gt:gt + gn, :, hh * 256:(hh + 1) * 256].rearrange("t p c -> p t c"),
                                                    in_=stgT[tg][:, 0:gn, 0:256]), reads=[("stgT", tg)])
                                else:
                                    for (gt, gn) in groups:
                                        tg = trot.next()
                                        for jj in range(gn):
                                            u0 = (gt + jj) * 128 - s0
                                            bk = bankrot.next() * 2 + (jj % 2)
                                            for k in range(KC):
                                                S.op("pe", lambda e, k=k, bk=bk, ws=ws, u0=u0, wcn=wcn: e.matmul(
                                                    pb[bk][:, 0:wcn], lhsT=uT[:, k, u0:u0 + 128], rhs=wsl[ws][:, k, 0:wcn],
                                                    start=(k == 0), stop=(k == KC - 1)),
                                                    reads=[("wsl", ws), ("uT", k), ("uTg", gt)], writes=[PB[bk]])
                                            fn = AF.Silu if si >= 102 else AF.Copy
                                            S.op("act", lambda e, bk=bk, tg=tg, jj=jj, fn=fn, wcn=wcn: e.activation(
                                                out=stgT[tg][:, jj, 0:wcn], in_=pb[bk][:, 0:wcn], func=fn),
                                                writes=[PB[bk], ("stgT", tg)])
                                        if si == 5:
                                            for g in range(2):
                                                S.dma(lambda e, tg=tg, gn=gn, gt=gt, g=g: e.dma_start(
                                                    out=av2[g, gt:gt + gn].rearrange("t p c -> p t c"),
                                                    in_=stgT[tg][:, 0:gn, g * 128:(g + 1) * 128]), reads=[("stgT", tg)])
                                                for jj in range(gn):
                                                    if (gt + jj) in HALO:
                                                        r0 = xblk(g, HALO[gt + jj], 1)
                                                        S.dma(lambda e, tg=tg, g=g, jj=jj, r0=r0: e.dma_start(
                                                            out=xs_att[r0:r0 + 128, :], in_=stgT[tg][:, jj, g * 128:(g + 1) * 128]),
                                                            reads=[("stgT", tg)], writes=[("xs_att", r0)])
                                        else:
                                            dst = rvM if si < 102 else rgM
                                            dc0 = (si % 2) * 512
                                            S.dma(lambda e, tg=tg, gn=gn, gt=gt, dc0=dc0, dst=dst: e.dma_start(
                                                out=dst[gt:gt + gn, :, dc0:dc0 + 512].rearrange("t p c -> p t c"),
                                                in_=stgT[tg][:, 0:gn, :]), reads=[("stgT", tg)])
                                    if si == 5:
                                        S.coll(lambda e: e.collective_compute("AllGather", ALU.bypass, replica_groups=RG,
                                                                              ins=[xs_att], outs=[xr_att]),
                                               reads=[("xs_att", r) for r in range(0, XA, 128)], writes=["xr_att"])
                            S.barrier()
                            S.emit()
                            checkpoint("p2")

                with ExitStack() as p2b:
                    kM4 = sb(p2b, "kM4", [128, 4, NT, 256], BF16)
                    v4 = sb(p2b, "v4", [128, 4, NT, 256], BF16)
                    zc4 = sb(p2b, "zc4", [128, 4, 4], F32)
                    St = sb(p2b, "St", [128, 8, 512], F32)
                    kzb = [sb(p2b, "kzb%d" % i, [128, 256], BF16) for i in range(6)]
                    kzrot, brot = Rot(6), Rot(6)
                    for hh in range(4):
                        lgf = sdexp[:, l * 16 + 8 + hh:l * 16 + 9 + hh]
                        lgb = sdexp[:, l * 16 + 12 + hh:l * 16 + 13 + hh]
                        for d_, (lg, col) in enumerate(((lgf, 0), (lgb, 1), (lgf, 2), (lgb, 2))):
                            S.op("act", lambda e, hh=hh, d_=d_, lg=lg, col=col: e.activation(
                                out=zc4[:, hh, d_:d_ + 1], in_=COLS[:, col:col + 1], func=AF.Exp, scale=lg),
                                reads=["cf", "sdexp"], writes=["zc4"])
                        for t0_ in range(0, NT, 8):
                            t1_ = min(NT, t0_ + 8)
                            S.dma(lambda e, hh=hh, t0_=t0_, t1_=t1_: e.dma_start(
                                out=kM4[:, hh, t0_:t1_, :], in_=rkM[t0_:t1_, :, hh * 256:(hh + 1) * 256].rearrange("t p c -> p t c")),
                                writes=[("kM4", hh)])
                            S.dma(lambda e, hh=hh, t0_=t0_, t1_=t1_: e.dma_start(
                                out=v4[:, hh, t0_:t1_, :], in_=rvM[t0_:t1_, :, hh * 256:(hh + 1) * 256].rearrange("t p c -> p t c")),
                                writes=[("v4", hh)])
                    S.op("pool", lambda e: e.memset(St[:], 0.0), writes=[("St", i) for i in range(8)])
                    for step in range(NT):
                        for hh in range(4):
                            for d_ in range(2):
                                tile_ = step if d_ == 0 else NT - 1 - step
                                if d_ == 1 and tile_ == 0:
                                    continue
                                kzs = kzrot.next()
                                bk = brot.next()
                                si_ = hh * 2 + d_
                                S.op("act", lambda e, hh=hh, d_=d_, tile_=tile_, kzs=kzs: e.activation(
                                    out=kzb[kzs][:], in_=kM4[:, hh, tile_, :], func=AF.Copy, scale=zc4[:, hh, d_:d_ + 1]),
                                    reads=[("kM4", hh), "zc4"], writes=[("kzb", kzs)])
                                for c in range(2):
                                    S.op("pe", lambda e, hh=hh, tile_=tile_, kzs=kzs, bk=bk, c=c: e.matmul(
                                        pb[bk][:, c * 256:(c + 1) * 256], lhsT=kzb[kzs][:, c * 128:(c + 1) * 128], rhs=v4[:, hh, tile_, :],
                                        start=True, stop=True), reads=[("kzb", kzs), ("v4", hh)], writes=[PB[bk]])
                                S.op("dve", lambda e, hh=hh, d_=d_, si_=si_, bk=bk: e.scalar_tensor_tensor(
                                    out=St[:, si_, :], in0=St[:, si_, :], scalar=zc4[:, hh, 2 + d_:3 + d_], in1=pb[bk][:, 0:512],
                                    op0=ALU.mult, op1=ALU.add), reads=["zc4"], writes=[PB[bk], ("St", si_)])
                    for si_ in range(8):
                        S.dma(lambda e, si_=si_: e.dma_start(out=xs_ret[si_ * 128:(si_ + 1) * 128, :], in_=St[:, si_, :]),
                              reads=[("St", si_)], writes=[("xs_ret", si_)])
                    S.coll(lambda e: e.collective_compute("AllGather", ALU.bypass, replica_groups=RG, ins=[xs_ret], outs=[xr_ret]),
                           reads=[("xs_ret", i) for i in range(8)], writes=["xr_ret"])
                    S.barrier(skip_coll=True)
                    S.emit()
                    checkpoint("p2b")

                quads = token_groups(0, NT)
                with ExitStack() as p3:
                    kTg = [sb(p3, "kTg%d" % g, [128, T], BF16) for g in range(2)]
                    vg = [sb(p3, "vg%d" % g, [128, NT, 128], BF16) for g in range(2)]
                    xk = [sb(p3, "xk%d" % g, [128, 3, 128], BF16) for g in range(2)]
                    xv = [sb(p3, "xv%d" % g, [128, 3, 128], BF16) for g in range(2)]
                    esrow = [sb(p3, "esrow%d" % g, [128, 512], F32) for g in range(2)]
                    q4 = [sb(p3, "q4_%d" % i, [128, 4, 512], BF16) for i in range(3)]
                    ost = [sb(p3, "ost%d" % i, [128, 4, 512], BF16) for i in range(2)]
                    E = [sb(p3, "E%d" % i, [128, 512], BF16) for i in range(10)]
                    den = [sb(p3, "den%d" % i, [128, 512], F32) for i in range(2)]
                    zero = sb(p3, "zero", [128, 512], F32)
                    S.op("pool", lambda e: e.memset(zero[:], 0.0), writes=["zero"])
                    erot, qrot, orot, drot, srot_, obrot = Rot(10), Rot(3), Rot(2), Rot(2), Rot(3), Rot(2)
                    for g in range(2):
                        S.dma(lambda e, g=g: e.dma_start(out=kTg[g][:], in_=akT[g]), writes=[("kTg", g)])
                        for t0_ in range(0, NT, 8):
                            t1_ = min(NT, t0_ + 8)
                            S.dma(lambda e, g=g, t0_=t0_, t1_=t1_: e.dma_start(out=vg[g][:, t0_:t1_, :],
                                                                               in_=av2[g, t0_:t1_].rearrange("t p c -> p t c")),
                                  writes=[("vg", g)])
                        for j_, (slot, ti) in enumerate(((0, 0), (0, 2), (1, 1))):
                            rk_ = slot * XA + xblk(g, ti, 0)
                            rv_ = slot * XA + xblk(g, ti, 1)
                            S.dma(lambda e, g=g, j_=j_, rk_=rk_: e.dma_start(out=xk[g][:, j_, :], in_=xr_att[rk_:rk_ + 128, :]),
                                  reads=["xr_att"], writes=[("xk", g)])
                            S.dma(lambda e, g=g, j_=j_, rv_=rv_: e.dma_start(out=xv[g][:, j_, :], in_=xr_att[rv_:rv_ + 128, :]),
                                  reads=["xr_att"], writes=[("xv", g)])
                        for hh in range(4):
                            ci = l * 16 + 4 * g + hh
                            S.op("dve", lambda e, g=g, hh=hh, ci=ci: e.tensor_scalar(out=esrow[g][:, hh * 128:(hh + 1) * 128], in0=zero[:, 0:128],
                                                                                     scalar1=sdexp[:, ci:ci + 1], scalar2=None, op0=ALU.add),
                                 reads=["zero", "sdexp"], writes=[("esrow", g)])
                    seq = [(g, n) for g in range(2) for n in range(NT)]
                    info = {}
                    qslot = {}

                    def att_stage1(g, n):
                        qt = (n // 4) * 4
                        qn = min(4, NT - qt)
                        jj = n - qt
                        if (g, qt) not in qslot:
                            qs = qrot.next()
                            qslot[(g, qt)] = qs
                            S.dma(lambda e: e.dma_start(
                                out=q4[qs][:, :, 0:qn * 128], in_=aqT[4 * g:4 * g + 4, :, qt * 128:(qt + qn) * 128].rearrange("h p n -> p h n")),
                                writes=[("q4", qs)])
                        qs = qslot[(g, qt)]
                        if n == 0:
                            kbl = [("X", 0, "meta"), ("L", 1, "le")]
                        elif n == 1:
                            kbl = [("X", 0, "meta"), ("X", 1, "xprev"), ("L", 1, None), ("L", 2, "le")]
                        elif n == NT - 1:
                            kbl = [("X", 0, "meta"), ("L", n - 1, "ge"), ("L", n, None), ("X", 2, "xnext")]
                        else:
                            kbl = [("X", 0, "meta"), ("L", n - 1, "ge"), ("L", n, None), ("L", n + 1, "le")]
                        if n == 1 and NT == 2:
                            kbl = [("X", 0, "meta"), ("X", 1, "xprev"), ("L", 1, None), ("X", 2, "xnext")]
                        lst = []
                        for (src, kb, m) in kbl:
                            bs = srot_.next()
                            es = erot.next()
                            if src == "L":
                                kap = kTg[g][:, kb * 128:(kb + 1) * 128]
                                vap = vg[g][:, kb, :]
                                kres, vres = ("kTg", g), ("vg", g)
                            else:
                                kap = xk[g][:, kb, :]
                                vap = xv[g][:, kb, :]
                                kres, vres = ("xk", g), ("xv", g)
                            S.op("pe", lambda e, bs=bs, kap=kap, qs=qs, jj=jj: e.matmul(
                                pb[bs][:, 0:512].rearrange("p (h n) -> p h n", h=4), lhsT=kap,
                                rhs=q4[qs][:, :, jj * 128:(jj + 1) * 128], start=True, stop=True),
                                reads=[kres, ("q4", qs)], writes=[PB[bs]])
                            S.op("act", lambda e, bs=bs, es=es: e.activation(out=E[es][:], in_=pb[bs][:, 0:512], func=AF.Exp,
                                                                             scale=float(128 ** -0.5)),
                                 writes=[PB[bs], ("E", es)])
                            if m is not None:
                                S.op("dve", lambda e, es=es, m=m: e.tensor_tensor(out=E[es][:], in0=E[es][:], in1=masks[m], op=ALU.mult),
                                     reads=["cbf"], writes=[("E", es)])
                            lst.append((es, vap, vres))
                        info[(g, n)] = lst

                    def att_stage2(g, n):
                        qt = (n // 4) * 4
                        qn = min(4, NT - qt)
                        jj = n - qt
                        lst = info.pop((g, n))
                        bo = 3 + obrot.next()
                        bd = 5
                        if jj == 0:
                            info[("os", g, qt)] = orot.next()
                        os_ = info[("os", g, qt)]
                        for idx, (es, vap, vres) in enumerate(lst):
                            last = idx == len(lst) - 1
                            S.op("pe", lambda e, bo=bo, vap=vap, es=es, idx=idx, last=last: e.matmul(
                                pb[bo][:, 0:512], lhsT=vap, rhs=E[es][:], start=(idx == 0), stop=last),
                                reads=[vres, ("E", es)], writes=[PB[bo]])
                            S.op("pe", lambda e, bd=bd, es=es, idx=idx, last=last: e.matmul(
                                pb[bd][:, 0:512], lhsT=ones, rhs=E[es][:], start=(idx == 0), stop=last),
                                reads=["cbf", ("E", es)], writes=[PB[bd]])
                        ds = drot.next()
                        S.op("dve", lambda e, ds=ds, bd=bd: e.tensor_tensor(out=den[ds][:], in0=pb[bd][:, 0:512], in1=esrow[g][:], op=ALU.add),
                             reads=[("esrow", g)], writes=[PB[bd], ("den", ds)])
                        S.op("dve", lambda e, ds=ds: e.reciprocal(out=den[ds][:], in_=den[ds][:]), writes=[("den", ds)])
                        S.op("dve", lambda e, ds=ds, bo=bo, os_=os_, jj=jj: e.tensor_tensor(
                            out=ost[os_][:, :, jj * 128:(jj + 1) * 128], in0=pb[bo][:, 0:512].rearrange("p (h n) -> p h n", h=4),
                            in1=den[ds][:].rearrange("p (h n) -> p h n", h=4), op=ALU.mult),
                            reads=[("den", ds)], writes=[PB[bo], ("ost", os_)])
                        if jj == qn - 1:
                            S.dma(lambda e, os_=os_: e.dma_start(
                                out=mixT[4 * g:4 * g + 4, :, qt * 128:(qt + qn) * 128].rearrange("h p n -> p h n"), in_=ost[os_][:, :, 0:qn * 128]),
                                reads=[("ost", os_)])

                    for i_ in range(len(seq) + 1):
                        if i_ < len(seq):
                            att_stage1(*seq[i_])
                        if i_ >= 1:
                            att_stage2(*seq[i_ - 1])
                    S.barrier(skip_coll=True)
                    S.emit()
                    checkpoint("p3a")

                pwo = ExitStack()
                wo = sb(pwo, "wo", [128, KC, D], BF16)
                for si in range(8):
                    S.dma(lambda e, si=si: e.dma_start(out=wo[:, :, si * 256:(si + 1) * 256],
                                                        in_=w_out[l].rearrange("(k p) c -> p k c", p=128)[:, :, si * 256:(si + 1) * 256]),
                          writes=[("wo", si)], q="pool")
                with ExitStack() as p3:
                    kTh = [sb(p3, "kTh%d" % i, [128, 2, T], BF16) for i in range(2)]
                    kMh = [sb(p3, "kMh%d" % i, [128, NT, 256], BF16) for i in range(2)]
                    vh = [sb(p3, "vh%d" % i, [128, NT, 256], BF16) for i in range(2)]
                    Rst = [sb(p3, "Rst%d" % i, [128, NT, 512], BF16) for i in range(2)]
                    DT = [sb(p3, "DT%d" % i, [128, 128], F32) for i in range(2)]
                    xi = [sb(p3, "xi%d" % i, [128, 2, 2, 128], BF16) for i in range(2)]
                    zc = [sb(p3, "zc%d" % i, [128, 4], F32) for i in range(2)]
                    Rb = [sb(p3, "Rb%d" % i, [128, 512], F32) for i in range(2)]
                    Sx = [sb(p3, "Sx%d" % i, [128, 512], F32) for i in range(2)]
                    tA = sb(p3, "tA", [128, 128], F32)
                    qq = [sb(p3, "qq%d" % i, [128, 2, 512], BF16) for i in range(3)]
                    gq = [sb(p3, "gq%d" % i, [128, 4, 256], BF16) for i in range(3)]
                    ost = [sb(p3, "rost%d" % i, [128, 2, 512], BF16) for i in range(2)]
                    Sf = sb(p3, "Sf", [128, 512], F32)
                    Sbf = [sb(p3, "Sbf%d" % i, [128, 512], BF16) for i in range(3)]
                    kz = [sb(p3, "kz%d" % i, [128, 256], BF16) for i in range(4)]
                    sm = [sb(p3, "sm%d" % i, [128, 128], BF16) for i in range(3)]
                    qf = [sb(p3, "qf%d" % i, [128, 2, 2, 128], BF16) for i in range(3)]
                    yv = [sb(p3, "yv%d" % i, [128, 256], BF16) for i in range(3)]
                    junk = sb(p3, "junk", [128, 256], F32)
                    ssr = [sb(p3, "ssr%d" % i, [128, 1], F32) for i in range(2)]
                    kzrot = Rot(4)
                    kvrot = Rot(2)

                    def head_setup(hh, p):
                        lgf = sdexp[:, l * 16 + 8 + hh:l * 16 + 9 + hh]
                        lgb = sdexp[:, l * 16 + 12 + hh:l * 16 + 13 + hh]
                        S.op("act", lambda e: e.activation(out=DT[p][:], in_=P1, func=AF.Exp, scale=lgf), reads=["cf", "sdexp"], writes=[("DT", p)])
                        S.op("dve", lambda e: e.tensor_tensor(out=DT[p][:], in0=DT[p][:], in1=MF, op=ALU.mult), reads=["cf"], writes=[("DT", p)])
                        S.op("act", lambda e: e.activation(out=tA[:], in_=P2, func=AF.Exp, scale=lgb), reads=["cf", "sdexp"], writes=["tA"])
                        S.op("dve", lambda e: e.tensor_tensor(out=tA[:], in0=tA[:], in1=MB, op=ALU.mult), reads=["cf"], writes=["tA"])
                        S.op("dve", lambda e: e.tensor_tensor(out=DT[p][:], in0=DT[p][:], in1=tA[:], op=ALU.add), reads=["tA"], writes=[("DT", p)])
                        for c in range(2):
                            S.op("act", lambda e, c=c: e.activation(out=xi[p][:, 0, c, :], in_=ROW1, func=AF.Exp, scale=lgf),
                                 reads=["cf", "sdexp"], writes=[("xi", p)])
                            S.op("act", lambda e, c=c: e.activation(out=xi[p][:, 1, c, :], in_=ROW2, func=AF.Exp, scale=lgb),
                                 reads=["cf", "sdexp"], writes=[("xi", p)])
                        for d_, (lg, col) in enumerate(((lgf, 0), (lgb, 1), (lgf, 2), (lgb, 2))):
                            S.op("act", lambda e, d_=d_, lg=lg, col=col: e.activation(
                                out=zc[p][:, d_:d_ + 1], in_=COLS[:, col:col + 1], func=AF.Exp, scale=lg),
                                reads=["cf", "sdexp"], writes=[("zc", p)])
                        S.dma(lambda e: e.dma_start(out=kTh[p][:], in_=rkT[2 * hh:2 * hh + 2].rearrange("c p n -> p c n")), writes=[("kTh", p)])
                        for t0_ in range(0, NT, 8):
                            t1_ = min(NT, t0_ + 8)
                            S.dma(lambda e, t0_=t0_, t1_=t1_: e.dma_start(
                                out=kMh[p][:, t0_:t1_, :], in_=rkM[t0_:t1_, :, hh * 256:(hh + 1) * 256].rearrange("t p c -> p t c")),
                                writes=[("kMh", p)])
                            S.dma(lambda e, t0_=t0_, t1_=t1_: e.dma_start(
                                out=vh[p][:, t0_:t1_, :], in_=rvM[t0_:t1_, :, hh * 256:(hh + 1) * 256].rearrange("t p c -> p t c")),
                                writes=[("vh", p)])
                        rb_ = XR + (hh * 2 + 1) * 128
                        sx_ = (hh * 2) * 128
                        S.dma(lambda e: e.dma_start(out=Rb[p][:], in_=xr_ret[rb_:rb_ + 128, :]), reads=["xr_ret"], writes=[("Rb", p)])
                        S.dma(lambda e: e.dma_start(out=Sx[p][:], in_=xr_ret[sx_:sx_ + 128, :]), reads=["xr_ret"], writes=[("Sx", p)])
                        S.op("dve", lambda e: e.tensor_scalar(out=Rb[p][:], in0=Rb[p][:], scalar1=COLS[:, 3:4], scalar2=None, op0=ALU.mult),
                             reads=["cf"], writes=[("Rb", p)])

                    def prepass_step(p, n):
                        kzs = kzrot.next()
                        bk = 2 + kvrot.next()
                        S.op("act", lambda e: e.activation(out=Rst[p][:, n, :], in_=Rb[p][:], func=AF.Copy), reads=[("Rb", p)], writes=[("Rst", p, n)])
                        S.op("act", lambda e: e.activation(out=kz[kzs][:], in_=kMh[p][:, n, :], func=AF.Copy, scale=zc[p][:, 1:2]),
                             reads=[("kMh", p), ("zc", p)], writes=[("kz", kzs)])
                        for c in range(2):
                            S.op("pe", lambda e, c=c: e.matmul(
                                pb[bk][:, c * 256:(c + 1) * 256], lhsT=kz[kzs][:, c * 128:(c + 1) * 128], rhs=vh[p][:, n, :],
                                start=True, stop=True), reads=[("kz", kzs), ("vh", p)], writes=[PB[bk]])
                        S.op("dve", lambda e: e.scalar_tensor_tensor(out=Rb[p][:], in0=Rb[p][:], scalar=zc[p][:, 3:4], in1=pb[bk][:, 0:512],
                                                                    op0=ALU.mult, op1=ALU.add),
                             reads=[("zc", p)], writes=[PB[bk], ("Rb", p)])

                    def quad_of(n):
                        qt = (n // 4) * 4
                        return qt, min(4, NT - qt), n - qt

                    def ret_A(hh, p, n, st_):
                        qt, qn, jj = quad_of(n)
                        if qt not in st_["q"]:
                            qs = (qt // 4) % 3
                            st_["q"][qt] = qs
                            S.dma(lambda e: e.dma_start(
                                out=qq[qs][:, :, 0:qn * 128], in_=rqT[2 * hh:2 * hh + 2, :, qt * 128:(qt + qn) * 128].rearrange("c p n -> p c n")),
                                writes=[("qq", qs)])
                        qs = st_["q"][qt]
                        s3 = n % 3
                        bsc = 4 + (n % 2)
                        bk = 2 + kvrot.next()
                        kzs = kzrot.next()
                        for c in range(2):
                            S.op("pe", lambda e, c=c: e.matmul(
                                pb[bsc][:, 0:128], lhsT=kTh[p][:, c, n * 128:(n + 1) * 128], rhs=qq[qs][:, c, jj * 128:(jj + 1) * 128],
                                start=(c == 0), stop=(c == 1)), reads=[("kTh", p), ("qq", qs)], writes=[PB[bsc]])
                        S.op("dve", lambda e: e.tensor_tensor(out=sm[s3][:], in0=pb[bsc][:, 0:128], in1=DT[p][:], op=ALU.mult),
                             reads=[("DT", p)], writes=[PB[bsc], ("sm", s3)])
                        for d_ in range(2):
                            S.op("dve", lambda e, d_=d_: e.tensor_tensor(
                                out=qf[s3][:, d_, :, :], in0=qq[qs][:, :, jj * 128:(jj + 1) * 128], in1=xi[p][:, d_, :, :], op=ALU.mult),
                                reads=[("qq", qs), ("xi", p)], writes=[("qf", s3)])
                        S.op("act", lambda e: e.activation(out=kz[kzs][:], in_=kMh[p][:, n, :], func=AF.Copy, scale=zc[p][:, 0:1]),
                             reads=[("kMh", p), ("zc", p)], writes=[("kz", kzs)])
                        for c in range(2):
                            S.op("pe", lambda e, c=c: e.matmul(
                                pb[bk][:, c * 256:(c + 1) * 256], lhsT=kz[kzs][:, c * 128:(c + 1) * 128], rhs=vh[p][:, n, :],
                                start=True, stop=True), reads=[("kz", kzs), ("vh", p)], writes=[PB[bk]])
                        S.op("dve", lambda e: e.scalar_tensor_tensor(out=Sf[:], in0=Sf[:], scalar=zc[p][:, 2:3], in1=pb[bk][:, 0:512],
                                                                    op0=ALU.mult, op1=ALU.add),
                             reads=[("zc", p)], writes=[PB[bk], "Sf"])
                        if n == 0:
                            S.op("dve", lambda e: e.scalar_tensor_tensor(out=Sf[:], in0=Sx[p][:], scalar=COLS[:, 4:5], in1=Sf[:],
                                                                        op0=ALU.mult, op1=ALU.add),
                                 reads=[("Sx", p), "cf"], writes=["Sf"])
                        nx = (n + 1) % 3
                        S.op("act", lambda e: e.activation(out=Sbf[nx][:], in_=Sf[:], func=AF.Copy), reads=["Sf"], writes=[("Sbf", nx)])

                    def ret_B(hh, p, n, st_):
                        qt, qn, jj = quad_of(n)
                        if qt not in st_["g"]:
                            gs = (qt // 4) % 3
                            st_["g"][qt] = gs
                            S.dma(lambda e: e.dma_start(
                                out=gq[gs][:, 0:qn, :], in_=rgM[qt:qt + qn, :, hh * 256:(hh + 1) * 256].rearrange("t p c -> p t c")),
                                writes=[("gq", gs)])
                        gs = st_["g"][qt]
                        s3 = n % 3
                        bo = n % 2
                        S.op("pe", lambda e: e.matmul(pb[bo][:, 0:256], lhsT=sm[s3][:], rhs=vh[p][:, n, :], start=True, stop=False),
                             reads=[("sm", s3), ("vh", p)], writes=[PB[bo]])
                        for c in range(2):
                            S.op("pe", lambda e, c=c: e.matmul(
                                pb[bo][:, 0:256], lhsT=qf[s3][:, 0, c, :], rhs=Sbf[s3][:, c * 256:(c + 1) * 256], start=False, stop=False),
                                reads=[("qf", s3), ("Sbf", s3)], writes=[PB[bo]])
                        for c in range(2):
                            S.op("pe", lambda e, c=c: e.matmul(
                                pb[bo][:, 0:256], lhsT=qf[s3][:, 1, c, :], rhs=Rst[p][:, n, c * 256:(c + 1) * 256], start=False, stop=(c == 1)),
                                reads=[("qf", s3), ("Rst", p, n)], writes=[PB[bo]])
                        sr = ssr[n % 2]
                        srn = ("ssr", n % 2)
                        S.op("act", lambda e: e.activation(out=junk[:], in_=pb[bo][:, 0:256], func=AF.Square, accum_out=sr[:, 0:1]),
                             writes=[PB[bo], "junk", srn])
                        S.op("act", lambda e: e.activation(out=sr[:, 0:1], in_=sr[:, 0:1], func=AF.Sqrt, scale=1.0 / 256, bias=epsc[:, 0:1]),
                             reads=["epsc"], writes=[srn])
                        S.op("dve", lambda e: e.reciprocal(out=sr[:, 0:1], in_=sr[:, 0:1]), writes=[srn])
                        S.op("dve", lambda e: e.scalar_tensor_tensor(
                            out=yv[s3][:], in0=pb[bo][:, 0:256], scalar=sr[:, 0:1], in1=gq[gs][:, jj, :], op0=ALU.mult, op1=ALU.mult),
                            reads=[srn, ("gq", gs)], writes=[PB[bo], ("yv", s3)])

                    def ret_C(hh, p, n, st_):
                        qt, qn, jj = quad_of(n)
                        if qt not in st_["o"]:
                            st_["o"][qt] = (qt // 4) % 2
                        os_ = st_["o"][qt]
                        s3 = n % 3
                        fb = n % 2
                        for c in range(2):
                            S.op("pe", lambda e, c=c: e.transpose(out=pbf[fb][:, c * 128:(c + 1) * 128],
                                                                  in_=yv[s3][:, c * 128:(c + 1) * 128], identity=ident),
                                 reads=[("yv", s3), "cbf"], writes=[PBF[fb]])
                        for c in range(2):
                            rc = l * 8 + 2 * hh + c
                            S.op("act", lambda e, c=c, rc=rc: e.activation(
                                out=ost[os_][:, c, jj * 128:(jj + 1) * 128], in_=pbf[fb][:, c * 128:(c + 1) * 128], func=AF.Copy,
                                scale=rn[:, rc:rc + 1]), reads=["rn"], writes=[PBF[fb], ("rost", os_)])
                        if jj == qn - 1:
                            S.dma(lambda e: e.dma_start(
                                out=mixT[8 + 2 * hh:8 + 2 * hh + 2, :, qt * 128:(qt + qn) * 128].rearrange("c p n -> p c n"),
                                in_=ost[os_][:, :, 0:qn * 128]), reads=[("rost", os_)])

                    head_setup(0, 0)
                    for n in range(NT - 1, -1, -1):
                        prepass_step(0, n)
                    for hh in range(4):
                        p = hh % 2
                        if hh + 1 < 4:
                            head_setup(hh + 1, 1 - p)
                        S.op("dve", lambda e: e.memset(Sf[:], 0.0), writes=["Sf"])
                        S.op("dve", lambda e: e.memset(Sbf[0][:], 0.0), writes=[("Sbf", 0)])
                        st_ = {"q": {}, "g": {}, "o": {}}
                        for i_ in range(NT + 2):
                            if i_ < NT:
                                ret_A(hh, p, i_, st_)
                            if 1 <= i_ <= NT:
                                ret_B(hh, p, i_ - 1, st_)
                            if i_ >= 2:
                                ret_C(hh, p, i_ - 2, st_)
                            if hh + 1 < 4 and i_ < NT:
                                prepass_step(1 - p, NT - 1 - i_)
                    S.barrier()
                    S.emit()
                    checkpoint("p3r")

                for (t0, t1) in supers:
                    groups = token_groups(t0, t1)
                    Ts5 = (t1 - t0) * 128
                    with ExitStack() as p4:
                        mg = [sb(p4, "mg%d" % i, [128, KC, 512], BF16) for i in range(2)]
                        yb2 = [sb(p4, "yb%d" % i, [128, KC, 512], BF16) for i in range(2)]
                        ph = {"hb": sb(p4, "hb", [128, KC, 512], F32), "yb": yb2[0], "ybn": "yb0",
                              "sqb": sb(p4, "sqb", [128, 4, 512], BF16), "rstd": sb(p4, "rstd", [128, 512], F32),
                              "tmpf": sb(p4, "tmpf", [128, 512], F32),
                              "ustg": [sb(p4, "ustg%d" % i, [128, 4, 512], BF16) for i in range(2)], "urot": Rot(2)}
                        mrot, brot = Rot(2), Rot(4)
                        rstdb = [ph["rstd"], sb(p4, "rstdB", [128, 512], F32)]
                        rstd2 = sb(p4, "rstd2", [128, 512], F32)
                        sqb2 = sb(p4, "sqb2", [128, 4, 512], BF16)
                        pending_tail = [None]
                        for gi, (gt, gn) in enumerate(groups):
                            N = gn * 128
                            c0 = gt * 128
                            ms = mrot.next()
                            ph["yb"] = yb2[ms]
                            ph["ybn"] = "yb%d" % ms
                            S.dma(lambda e, ms=ms, c0=c0, N=N: e.dma_start(out=mg[ms][:, :, 0:N], in_=mixT[:, :, c0:c0 + N].rearrange("k p n -> p k n")),
                                  writes=[("mg", ms)])
                            for dch in range(KC):
                                bk = brot.next()
                                for k in range(KC):
                                    S.op("pe", lambda e, bk=bk, k=k, dch=dch, ms=ms, N=N: e.matmul(
                                        pb[bk][:, 0:N], lhsT=wo[:, k, dch * 128:(dch + 1) * 128], rhs=mg[ms][:, k, 0:N],
                                        start=(k == 0), stop=(k == KC - 1)), reads=[("wo", dch // 2), ("mg", ms)], writes=[PB[bk]])
                                evac_y(ph, bk, dch, N, l, 1)
                                if dch == 3 and pending_tail[0] is not None:
                                    pending_tail[0]()
                                    pending_tail[0] = None
                            rs = gi % 2
                            flush_ss()
                            rstd_from_ss(pb[5][:, 0:N], rstdb[rs][:, 0:N], 1.0 / D, ("rstd", rs), PB[5],
                                         extra_mask=(TMASK[:, 0:128] if c0 == 0 else None))

                            def tail(c0=c0, N=N, yb=ph["yb"], ybn=ph["ybn"], rs=rs):
                                hb = ph["hb"]
                                tmpf = ph["tmpf"]
                                sqb = ph["sqb"]
                                ustg = ph["ustg"]
                                rs_ap = rstdb[rs]
                                for q4 in range(4):
                                    S.dma(lambda e, q4=q4: e.dma_start(out=hb[:, q4 * 4:(q4 + 1) * 4, 0:N],
                                                                        in_=h_in[q4 * 4:(q4 + 1) * 4, :, c0:c0 + N].rearrange("k p n -> p k n")),
                                          writes=[("hb", q4)])
                                for k in range(KC):
                                    S.op("dve", lambda e, k=k: e.tensor_tensor(out=tmpf[:, 0:N], in0=yb[:, k, 0:N], in1=rs_ap[:, 0:N], op=ALU.mult),
                                         reads=[("rstd", rs), (ybn, k)], writes=["tmpf"])
                                    S.op("dve", lambda e, k=k: e.tensor_tensor(out=hb[:, k, 0:N], in0=hb[:, k, 0:N], in1=tmpf[:, 0:N], op=ALU.add),
                                         reads=["tmpf"], writes=[("hb", k // 4)])
                                for q4 in range(4):
                                    S.dma(lambda e, q4=q4: e.dma_start(out=h_mid[q4 * 4:(q4 + 1) * 4, :, c0:c0 + N].rearrange("k p n -> p k n"),
                                                                        in_=hb[:, q4 * 4:(q4 + 1) * 4, 0:N]),
                                          reads=[("hb", q4)])
                                for k in range(KC):
                                    sl = k % 4
                                    S.op("act", lambda e, k=k, sl=sl: e.activation(out=sqb2[:, sl, 0:N], in_=hb[:, k, 0:N], func=AF.Square),
                                         reads=[("hb", k // 4)], writes=[("sqb2", sl)])
                                    S.op("pe", lambda e, k=k, sl=sl: e.matmul(pb[4][:, 0:N], lhsT=ones, rhs=sqb2[:, sl, 0:N],
                                                                              start=(k == 0), stop=(k == KC - 1)),
                                         reads=[("sqb2", sl), "cbf"], writes=[PB[4]])
                                rstd_from_ss(pb[4][:, 0:N], rstd2[:, 0:N], 1.0 / D, "rstd2", PB[4])
                                for q4 in range(4):
                                    us = ph["urot"].next()
                                    for kk in range(4):
                                        k = q4 * 4 + kk
                                        S.op("dve", lambda e, k=k, kk=kk, us=us: e.scalar_tensor_tensor(
                                            out=ustg[us][:, kk, 0:N], in0=hb[:, k, 0:N], scalar=gcol(l, 2, k), in1=rstd2[:, 0:N],
                                            op0=ALU.mult, op1=ALU.mult), reads=[("hb", q4), "rstd2", "gains"], writes=[("ustg", us)])
                                    S.dma(lambda e, q4=q4, us=us: e.dma_start(
                                        out=u2T_d[q4 * 4:(q4 + 1) * 4, :, c0:c0 + N].rearrange("k p n -> p k n"), in_=ustg[us][:, :, 0:N]),
                                        reads=[("ustg", us)])
                            pending_tail[0] = tail
                        if pending_tail[0] is not None:
                            pending_tail[0]()
                        S.barrier()
                        S.emit()
                        checkpoint("p4")
                    pwo.close()
                    with ExitStack() as p5ab:
                        u2T = sb(p5ab, "u2T", [128, KC, Ts5], BF16)
                        for (gt, gn) in groups:
                            S.dma(lambda e, gt=gt, gn=gn: e.dma_start(
                                out=u2T[:, :, gt * 128 - t0 * 128:(gt + gn) * 128 - t0 * 128],
                                in_=u2T_d[:, :, gt * 128:(gt + gn) * 128].rearrange("k p n -> p k n")),
                                writes=[("u2Tg", gt)])
                        with ExitStack() as p5b:
                            wgu = [sb(p5b, "wgu%d" % i, [128, KC, 1024], BF16) for i in range(2)]
                            sg = [sb(p5b, "sg%d" % i, [128, 512], BF16) for i in range(2)]
                            fst = [sb(p5b, "fst%d" % i, [128, 512], BF16) for i in range(4)]
                            wrot, brot, grot, frot = Rot(2), Rot(2), Rot(2), Rot(4)
                            for fs_ in range(FC // 4):
                                ws = wrot.next()
                                for hf in range(2):
                                    S.dma(lambda e, ws=ws, fs_=fs_, hf=hf: e.dma_start(
                                        out=wgu[ws][:, hf * 8:(hf + 1) * 8, 0:512],
                                        in_=w_gate[l].rearrange("(k p) c -> p k c", p=128)[:, hf * 8:(hf + 1) * 8, fs_ * 512:(fs_ + 1) * 512]),
                                        writes=[("wgu", ws, 0, hf)], q="pool")
                                for hf in range(2):
                                    S.dma(lambda e, ws=ws, fs_=fs_, hf=hf: e.dma_start(
                                        out=wgu[ws][:, hf * 8:(hf + 1) * 8, 512:1024],
                                        in_=w_up[l].rearrange("(k p) c -> p k c", p=128)[:, hf * 8:(hf + 1) * 8, fs_ * 512:(fs_ + 1) * 512]),
                                        writes=[("wgu", ws, 1, hf)], q="pool")
                                for (gt, gn) in groups:
                                    N = gn * 128
                                    u0 = gt * 128 - t0 * 128
                                    for fi in range(4):
                                        f = fs_ * 4 + fi
                                        bp = brot.next() * 2
                                        for c in range(2):
                                            for k in range(KC):
                                                S.op("pe", lambda e, c=c, k=k, bp=bp, ws=ws, N=N, fi=fi, u0=u0: e.matmul(
                                                    pb[bp + c][:, 0:N], lhsT=wgu[ws][:, k, c * 512 + fi * 128:c * 512 + (fi + 1) * 128],
                                                    rhs=u2T[:, k, u0:u0 + N], start=(k == 0), stop=(k == KC - 1)),
                                                    reads=[("wgu", ws, c, k // 8), ("u2Tg", gt)], writes=[PB[bp + c]])
                                        gs = grot.next()
                                        fsl = frot.next()
                                        S.op("act", lambda e, bp=bp, gs=gs, N=N: e.activation(out=sg[gs][:, 0:N], in_=pb[bp][:, 0:N], func=AF.Silu),
                                             writes=[PB[bp], ("sg", gs)])
                                        S.op("dve", lambda e, bp=bp, gs=gs, fsl=fsl, N=N: e.tensor_tensor(
                                            out=fst[fsl][:, 0:N], in0=pb[bp + 1][:, 0:N], in1=sg[gs][:, 0:N], op=ALU.mult),
                                            reads=[("sg", gs)], writes=[PB[bp + 1], ("fst", fsl)])
                                        S.dma(lambda e, fsl=fsl, f=f, gt=gt, N=N: e.dma_start(out=fT_d[f, :, gt * 128:gt * 128 + N], in_=fst[fsl][:, 0:N]),
                                              reads=[("fst", fsl)])
                            S.barrier()
                            S.emit()
                            checkpoint("p5b")
                    with ExitStack() as p5:
                        GM = 6
                        NM = GM * 128
                        fT = sb(p5, "fT", [128, FC, NM], BF16)
                        wd = [sb(p5, "wd%d" % i, [128, FC, 256], BF16) for i in range(2)]
                        hb = sb(p5, "hb", [128, KC, NM], F32)
                        yb = sb(p5, "yb", [128, KC, NM], BF16)
                        sqb = sb(p5, "sqb", [128, 4, NM], BF16)
                        rstd = sb(p5, "rstd", [128, NM], F32)
                        tmpf = sb(p5, "tmpf", [128, NM], F32)
                        ustg = [sb(p5, "ustg%d" % i, [128, 1, NM], BF16) for i in range(2)]
                        urot = Rot(2)
                        drot, brot = Rot(2), Rot(2)
                        for (gt, gn) in token_groups(t0, t1, gmax=GM):
                            N = gn * 128
                            c0 = gt * 128
                            segs = [(0, min(N, 512), 0)]
                            if N > 512:
                                segs.append((512, N - 512, 1))
                            for q4 in range(4):
                                S.dma(lambda e, q4=q4, c0=c0, N=N: e.dma_start(
                                    out=fT[:, q4 * 11:(q4 + 1) * 11, 0:N],
                                    in_=fT_d[q4 * 11:(q4 + 1) * 11, :, c0:c0 + N].rearrange("k p n -> p k n")),
                                    writes=[("fT", q4)])
                            pend = []
                            for ds_ in range(8):
                                dsl = drot.next()
                                for (k0, k1) in ((0, 22), (22, 44)):
                                    S.dma(lambda e, dsl=dsl, ds_=ds_, k0=k0, k1=k1: e.dma_start(
                                        out=wd[dsl][:, k0:k1, :],
                                        in_=w_down[l].rearrange("(k p) c -> p k c", p=128)[:, k0:k1, ds_ * 256:(ds_ + 1) * 256]),
                                        writes=[("wd", dsl, k0)], q="pool")
                                for c in range(2):
                                    dch = ds_ * 2 + c
                                    br = brot.next()
                                    for k in range(FC):
                                        for (sc, sn, sbk) in segs:
                                            bk = br + 2 * sbk
                                            S.op("pe", lambda e, bk=bk, k=k, c=c, dsl=dsl, sc=sc, sn=sn: e.matmul(
                                                pb[bk][:, 0:sn], lhsT=wd[dsl][:, k, c * 128:(c + 1) * 128], rhs=fT[:, k, sc:sc + sn],
                                                start=(k == 0), stop=(k == FC - 1)),
                                                reads=[("wd", dsl, 0 if k < 22 else 22), ("fT", k // 11)], writes=[PB[bk]])
                                    sl = dch % 4
                                    for (sc, sn, sbk) in segs:
                                        bk = br + 2 * sbk
                                        S.op("act", lambda e, bk=bk, dch=dch, sc=sc, sn=sn: e.activation(
                                            out=yb[:, dch, sc:sc + sn], in_=pb[bk][:, 0:sn], func=AF.Copy, scale=gcol(l, 3, dch)),
                                            reads=["gains"], writes=[PB[bk], ("yb", dch, sbk)])
                                        S.op("act", lambda e, bk=bk, sl=sl, sc=sc, sn=sn: e.activation(
                                            out=sqb[:, sl, sc:sc + sn], in_=pb[bk][:, 0:sn], func=AF.Square),
                                            writes=[PB[bk], ("sqb", sl, sbk)])

                                        def ssmm(dch=dch, sl=sl, sc=sc, sn=sn, sbk=sbk):
                                            S.op("pe", lambda e: e.matmul(pb[4 + sbk][:, 0:sn], lhsT=ones, rhs=sqb[:, sl, sc:sc + sn],
                                                                          start=(dch == 0), stop=(dch == KC - 1)),
                                                 reads=[("sqb", sl, sbk), "cbf"], writes=[PB[4 + sbk]])
                                        pend.append(ssmm)
                                    while len(pend) > 2 * len(segs):
                                        pend.pop(0)()
                            while pend:
                                pend.pop(0)()
                            for (sc, sn, sbk) in segs:
                                S.op("act", lambda e, sc=sc, sn=sn, sbk=sbk: e.activation(
                                    out=rstd[:, sc:sc + sn], in_=pb[4 + sbk][:, 0:sn], func=AF.Sqrt, scale=1.0 / D, bias=epsc[:, 0:1]),
                                    reads=["epsc"], writes=[PB[4 + sbk], "rstd"])
                            S.op("dve", lambda e, N=N: e.reciprocal(out=rstd[:, 0:N], in_=rstd[:, 0:N]), writes=["rstd"])
                            if c0 == 0:
                                S.op("dve", lambda e: e.tensor_tensor(out=rstd[:, 0:128], in0=rstd[:, 0:128], in1=TMASK[:, 0:128], op=ALU.mult),
                                     reads=["cf"], writes=["rstd"])
                            for q4 in range(4):
                                S.dma(lambda e, q4=q4, c0=c0, N=N: e.dma_start(
                                    out=hb[:, q4 * 4:(q4 + 1) * 4, 0:N],
                                    in_=h_mid[q4 * 4:(q4 + 1) * 4, :, c0:c0 + N].rearrange("k p n -> p k n")),
                                    writes=[("hb", q4)])
                            for k in range(KC):
                                S.op("dve", lambda e, k=k, N=N: e.tensor_tensor(out=tmpf[:, 0:N], in0=yb[:, k, 0:N], in1=rstd[:, 0:N], op=ALU.mult),
                                     reads=["rstd", ("yb", k, 0), ("yb", k, 1)], writes=["tmpf"])
                                S.op("dve", lambda e, k=k, N=N: e.tensor_tensor(out=hb[:, k, 0:N], in0=hb[:, k, 0:N], in1=tmpf[:, 0:N], op=ALU.add),
                                     reads=["tmpf"], writes=[("hb", k // 4)])
                            for q4 in range(4):
                                S.dma(lambda e, q4=q4, c0=c0, N=N: e.dma_start(
                                    out=h_out[q4 * 4:(q4 + 1) * 4, :, c0:c0 + N].rearrange("k p n -> p k n"),
                                    in_=hb[:, q4 * 4:(q4 + 1) * 4, 0:N]),
                                    reads=[("hb", q4)])
                            if l < L - 1:
                                for k in range(KC):
                                    sl = k % 4
                                    for (sc, sn, sbk) in segs:
                                        S.op("act", lambda e, k=k, sl=sl, sc=sc, sn=sn: e.activation(
                                            out=sqb[:, sl, sc:sc + sn], in_=hb[:, k, sc:sc + sn], func=AF.Square),
                                            reads=[("hb", k // 4)], writes=[("sqb", sl, sbk)])
                                        S.op("pe", lambda e, k=k, sl=sl, sc=sc, sn=sn, sbk=sbk: e.matmul(
                                            pb[4 + sbk][:, 0:sn], lhsT=ones, rhs=sqb[:, sl, sc:sc + sn], start=(k == 0), stop=(k == KC - 1)),
                                            reads=[("sqb", sl, sbk), "cbf"], writes=[PB[4 + sbk]])
                                for (sc, sn, sbk) in segs:
                                    S.op("act", lambda e, sc=sc, sn=sn, sbk=sbk: e.activation(
                                        out=rstd[:, sc:sc + sn], in_=pb[4 + sbk][:, 0:sn], func=AF.Sqrt, scale=1.0 / D, bias=epsc[:, 0:1]),
                                        reads=["epsc"], writes=[PB[4 + sbk], "rstd"])
                                S.op("dve", lambda e, N=N: e.reciprocal(out=rstd[:, 0:N], in_=rstd[:, 0:N]), writes=["rstd"])
                                for k in range(KC):
                                    us = urot.next()
                                    S.op("dve", lambda e, k=k, us=us, N=N: e.scalar_tensor_tensor(
                                        out=ustg[us][:, 0, 0:N], in0=hb[:, k, 0:N], scalar=gcol(l + 1, 0, k), in1=rstd[:, 0:N],
                                        op0=ALU.mult, op1=ALU.mult), reads=[("hb", k // 4), "rstd", "gains"], writes=[("ustg", us)])
                                    S.dma(lambda e, k=k, us=us, c0=c0, N=N: e.dma_start(
                                        out=uT_d[k, :, c0:c0 + N], in_=ustg[us][:, 0, 0:N]),
                                        reads=[("ustg", us)])
                        S.barrier()
                        S.emit()
                        checkpoint("p5")
        except _Stop:
            pass
        S.barrier()
        S.emit()
    return nc


def make_consts(T, half, tok_off):
    pos = (np.arange(T) + tok_off - PADF).astype(np.float32)
    inv = (10000.0 ** (-np.arange(128, dtype=np.float32) / 128)).astype(np.float32)
    ang = pos[None, :] * inv[:, None]
    cr, sr = np.cos(ang).astype(np.float32), np.sin(ang).astype(np.float32)
    ropeR = np.stack([cr, sr, cr / 16.0, sr / 16.0]).astype(np.float32)
    inva = (500000.0 ** (-np.arange(16, dtype=np.float32) / 16)).astype(np.float32)
    anga = pos[None, :] * inva[:, None]
    ca, sa = np.cos(anga).astype(np.float32), np.sin(anga).astype(np.float32)
    ropeA = np.stack([np.concatenate([ca, ca], 0), np.concatenate([-sa, sa], 0)]).astype(np.float32)
    cbf = np.zeros((128, 128 * 2 + 32 + 512 * 5), np.float32)
    cbf[:, 0:128] = 1.0
    cbf[:, 128:256] = np.eye(128)
    for m in range(32):
        src = m + 16 if m < 16 else m - 16
        cbf[src, 256 + m] = 1.0
    lk = np.arange(128)[:, None]
    lq = np.arange(128)[None, :]
    ge = np.tile((lk >= lq), (1, 4)).astype(np.float32)
    le = np.tile((lk <= lq), (1, 4)).astype(np.float32)
    cbf[:, 288:800] = np.tile((lk >= PADF) * np.ones((1, 128)), (1, 4))
    cbf[:, 800:1312] = ge
    cbf[:, 1312:1824] = le
    if half == 1:
        cbf[:, 1824:2336] = ge
    else:
        cbf[:, 2336:2848] = le
    cf = np.zeros((128, 128 * 6 + 8 + 128), np.float32)
    j = np.arange(128)[:, None].astype(np.float32)
    i = np.arange(128)[None, :].astype(np.float32)
    cf[:, 0:128] = np.maximum(i - j, 0)
    cf[:, 128:256] = (i >= j)
    cf[:, 256:384] = np.maximum(j - i, 0)
    cf[:, 384:512] = (j > i)
    cf[:, 512:640] = i + 1.0
    cf[:, 640:768] = 128.0 - i
    cf[:, 768] = 127.0 - np.arange(128)
    cf[:, 769] = np.arange(128)
    cf[:, 770] = 128.0
    cf[:, 771] = 1.0 - half
    cf[:, 772] = float(half)
    if half == 0:
        cf[:, 776:904] = (i >= PADF) * np.ones((128, 1))
    return ropeR, ropeA, cbf, cf


def kernel(x, meta_tokens, w_in, w_out, attn_sink, ret_decay_fwd, ret_decay_bwd, ret_norm,
           norm_mix_pre, norm_mix_post, w_gate, w_up, w_down, norm_ffn_pre, norm_ffn_post, _L=None):
    x = np.asarray(x, np.float32)
    B, SEQ, _ = x.shape
    L = int(_L) if _L is not None else w_in.shape[0]
    HS = SEQ // 2
    NT = HS // 128 + 1
    T = NT * 128
    f32 = lambda a: np.ascontiguousarray(np.asarray(a, np.float32))
    g4 = np.stack([f32(norm_mix_pre)[:L], f32(norm_mix_post)[:L], f32(norm_ffn_pre)[:L], f32(norm_ffn_post)[:L]], 1)
    gains = np.ascontiguousarray(g4.reshape(L, 4, KC, 128).transpose(3, 0, 1, 2).reshape(128, L * 4 * KC))
    rnl = np.ascontiguousarray(f32(ret_norm)[:L].reshape(L, 8, 128).transpose(2, 0, 1).reshape(128, L * 8))
    sd = np.concatenate([f32(attn_sink)[:L], f32(ret_decay_fwd)[:L], f32(ret_decay_bwd)[:L]], 1).reshape(1, L * 16)
    sd = np.ascontiguousarray(np.broadcast_to(sd, (128, L * 16)))
    shared = {
        "w_in": f32(w_in)[:L], "w_out": f32(w_out)[:L], "w_gate": f32(w_gate)[:L], "w_up": f32(w_up)[:L], "w_down": f32(w_down)[:L],
        "gains": gains, "retnorm": rnl, "sinkdecay": sd,
    }
    mt = f32(meta_tokens)
    consts = [make_consts(T, 0, 0), make_consts(T, 1, HS)]
    in_maps = []
    for b in range(B):
        for half in range(2):
            m = dict(shared)
            h0 = np.zeros((T, D), np.float32)
            if half == 0:
                h0[PADF:PADF + N_META] = mt
            h0[128:] = x[b, half * HS:(half + 1) * HS]
            m["h0T"] = np.ascontiguousarray(h0.T.reshape(KC, 128, T))
            m["ropeR"], m["ropeA"], m["cbf"], m["cf"] = consts[half]
            in_maps.append(m)
    nc = build_program(L=L, NT=NT)
    res = run_bass_kernel_spmd(nc, in_maps, core_ids=list(range(2 * B)))
    out = np.empty((B, SEQ, D), np.float32)
    for b in range(B):
        for half in range(2):
            oT = np.asarray(res.results[2 * b + half]["outT"]).reshape(D, T)
            out[b, half * HS:(half + 1) * HS] = oT[:, 128:].T
    return out
```
